# Optimizing a Trainium2 kernel written in Bass

```python
import math
import jax, jax.numpy as jnp
from jax import lax
import numpy as np

D_MODEL = 1024
BATCH = 8
SEQ = 4096
DEPTH = 2
DEC_BATCH = 8
DEC_SEQ = 64
PAST_LEN = 1024

CHUNK = 64
N_META = 16
SCAN_BLOCK = CHUNK
MIX_WIDTH = D_MODEL
SSM_WIDTH = MIX_WIDTH // 2
SSM_HEADDIM = 64
SSM_HEADS = SSM_WIDTH // SSM_HEADDIM
SSM_GROUPS = 2
SSM_STATE = 128
CONV_WIDTH = 4
CONV_CH = SSM_WIDTH + 2 * SSM_GROUPS * SSM_STATE
HG_WIDTH = MIX_WIDTH - SSM_WIDTH
HG_EXPAND = 128
HG_HEADS = HG_WIDTH // HG_EXPAND
HG_VDIM = HG_WIDTH // HG_HEADS
IN_SPLITS = tuple(np.cumsum([SSM_WIDTH, CONV_CH, SSM_HEADS, HG_WIDTH, HG_WIDTH, HG_WIDTH]).tolist())
IN_COLS = SSM_WIDTH + CONV_CH + SSM_HEADS + 4 * HG_WIDTH
D_FF = 2816
EPS = 1e-6
LB_FLOOR = 1e-30

kernel_name = "hymba_ssd_hgrn2_macaron_stream"


def _rmsnorm(x, w, groups=1):
    xf = x.astype(jnp.float32)
    shp = xf.shape
    xg = xf.reshape(shp[:-1] + (groups, shp[-1] // groups))
    xg = xg * lax.rsqrt(jnp.mean(xg * xg, axis=-1, keepdims=True) + EPS)
    return (xg.reshape(shp) * w.astype(jnp.float32)).astype(x.dtype)


def _swiglu(x, w_gate, w_up, w_down):
    return (jax.nn.silu(x @ w_gate) * (x @ w_up)) @ w_down


def _causal_conv(u, buf, w, b):
    L = u.shape[1]
    xp = jnp.concatenate([buf.astype(u.dtype), u], axis=1)
    out = b + xp[:, 0:L] * w[0]
    for k in range(1, CONV_WIDTH):
        out = out + xp[:, k:k + L] * w[k]
    return out, xp[:, xp.shape[1] - (CONV_WIDTH - 1):]


def _pad(a, Lp):
    return jnp.pad(a, [(0, 0), (0, Lp - a.shape[1])] + [(0, 0)] * (a.ndim - 2))


def _to_blocks(a, nb):
    bt = a.shape[0]
    return jnp.moveaxis(a.reshape((bt, nb, SCAN_BLOCK) + a.shape[2:]), 1, 0)


def _masked_decay(seg, causal):
    return jnp.where(causal, jnp.exp(jnp.where(causal, seg, 0.0)), 0.0)


def _ssd_scan(xdt, la, bm, cm, s0):
    bt, L, H, P = xdt.shape
    G, N = bm.shape[2], bm.shape[3]
    R = H // G
    nb = L // SCAN_BLOCK
    causal = jnp.tril(jnp.ones((SCAN_BLOCK, SCAN_BLOCK), bool))[None, :, :, None, None]
    xs = (_to_blocks(xdt.reshape(bt, L, G, R, P), nb), _to_blocks(la.reshape(bt, L, G, R), nb),
          _to_blocks(bm, nb), _to_blocks(cm, nb))

    def step(S, blk):
        xb, lab, bb, cb = blk
        cum = jnp.cumsum(lab, axis=1)
        seg = cum[:, :, None] - cum[:, None]
        decay = _masked_decay(seg, causal)
        cbm = jnp.einsum('bign,bjgn->bijg', cb, bb)
        y = jnp.einsum('bijg,bijgr,bjgrp->bigrp', cbm, decay, xb)
        y = y + jnp.einsum('bign,bgrpn,bigr->bigrp', cb, S, jnp.exp(cum))
        last = cum[:, -1]
        S = jnp.exp(last)[..., None, None] * S + jnp.einsum(
            'bjgn,bjgr,bjgrp->bgrpn', bb, jnp.exp(last[:, None] - cum), xb)
        return S, y

    S, ys = lax.scan(step, s0.reshape(bt, G, R, P, N), xs)
    y = jnp.moveaxis(ys, 0, 1).reshape(bt, L, H, P)
    return y, S.reshape(bt, H, P, N)


def _gla_scan(q, k, v, lf, s0):
    bt, L, H, K = q.shape
    nb = L // SCAN_BLOCK
    causal = jnp.tril(jnp.ones((SCAN_BLOCK, SCAN_BLOCK), bool))[None, :, :, None, None]
    xs = (_to_blocks(q, nb), _to_blocks(k, nb), _to_blocks(v, nb), _to_blocks(lf, nb))

    def step(S, blk):
        qb, kb, vb, fb = blk
        cum = jnp.cumsum(fb, axis=1)
        seg = cum[:, :, None] - cum[:, None]
        decay = _masked_decay(seg, causal)
        att = jnp.einsum('bihk,bjhk,bijhk->bijh', qb, kb, decay)
        o = jnp.einsum('bijh,bjhv->bihv', att, vb) + jnp.einsum('bihk,bhkv->bihv', qb * jnp.exp(cum), S)
        last = cum[:, -1]
        S = jnp.exp(last)[..., None] * S + jnp.einsum(
            'bjhk,bjhv->bhkv', kb * jnp.exp(last[:, None] - cum), vb)
        return S, o

    S, os_ = lax.scan(step, s0, xs)
    return jnp.moveaxis(os_, 0, 1).reshape(bt, L, H, v.shape[-1]), S


def _mixer(hn, conv_buf, ssm_s, hg_s, w_in, conv_w, conv_b, dt_bias, a_log, d_skip,
           ssm_norm_w, hg_lb, hg_norm_w, w_out):
    f32 = jnp.float32
    bt, L, _ = hn.shape
    Lp = -(-L // SCAN_BLOCK) * SCAN_BLOCK
    proj = hn @ w_in
    z, xbc, dt_raw, hq, hf, hi, hg = jnp.split(proj, IN_SPLITS, axis=-1)

    xbc, new_conv = _causal_conv(xbc, conv_buf, conv_w, conv_b)
    xbc = jax.nn.silu(xbc).astype(f32)
    xs, bm, cm = jnp.split(xbc, [SSM_WIDTH, SSM_WIDTH + SSM_GROUPS * SSM_STATE], axis=-1)
    xs = xs.reshape(bt, L, SSM_HEADS, SSM_HEADDIM)
    bm = bm.reshape(bt, L, SSM_GROUPS, SSM_STATE)
    cm = cm.reshape(bt, L, SSM_GROUPS, SSM_STATE)
    dt = jax.nn.softplus(dt_raw.astype(f32) + dt_bias.astype(f32))
    a = -jnp.exp(a_log.astype(f32))
    y, new_ssm = _ssd_scan(_pad(xs * dt[..., None], Lp), _pad(dt * a, Lp), _pad(bm, Lp), _pad(cm, Lp),
                           ssm_s.astype(f32))
    y = y[:, :L] + d_skip.astype(f32)[:, None] * xs
    y = y.reshape(bt, L, SSM_WIDTH) * jax.nn.silu(z.astype(f32))
    y_ssm = _rmsnorm(y, ssm_norm_w, SSM_GROUPS)

    q = jax.nn.silu(hq.astype(f32)).reshape(bt, L, HG_HEADS, HG_EXPAND)
    fr = hf.astype(f32)
    lf = jnp.logaddexp(jnp.log(jnp.maximum(hg_lb, LB_FLOOR)), jnp.log1p(-hg_lb) + jax.nn.log_sigmoid(fr))
    kk = (1.0 - hg_lb) * jax.nn.sigmoid(-fr)
    v = hi.astype(f32).reshape(bt, L, HG_HEADS, HG_VDIM)
    o, new_hg = _gla_scan(_pad(q, Lp), _pad(kk.reshape(bt, L, HG_HEADS, HG_EXPAND), Lp), _pad(v, Lp),
                          _pad(lf.reshape(bt, L, HG_HEADS, HG_EXPAND), Lp), hg_s.astype(f32))
    o = _rmsnorm(o[:, :L], hg_norm_w.reshape(HG_HEADS, HG_VDIM))
    o = o.reshape(bt, L, HG_WIDTH) * jax.nn.silu(hg.astype(f32))

    mixed = jnp.concatenate([y_ssm, o], axis=-1).astype(hn.dtype) @ w_out
    return (mixed, new_conv.astype(conv_buf.dtype), new_ssm.astype(ssm_s.dtype), new_hg.astype(hg_s.dtype))


def _trunk(h, conv_st, ssm_st, hg_st, p):
    (ln_ffa_w, ffa_w_gate, ffa_w_up, ffa_w_down, ln_mix_w, w_in, conv_w, conv_b, dt_bias, a_log, d_skip,
     ssm_norm_w, hg_lb, hg_norm_w, w_out, ln_ffb_w, ffb_w_gate, ffb_w_up, ffb_w_down, ln_f_w) = p
    conv_new, ssm_new, hg_new = [], [], []
    for l in range(DEPTH):
        h = h + 0.5 * _swiglu(_rmsnorm(h, ln_ffa_w[l]), ffa_w_gate[l], ffa_w_up[l], ffa_w_down[l])
        m, c, s, g = _mixer(_rmsnorm(h, ln_mix_w[l]), conv_st[l], ssm_st[l], hg_st[l], w_in[l], conv_w[l],
                            conv_b[l], dt_bias[l], a_log[l], d_skip[l], ssm_norm_w[l], hg_lb[l], hg_norm_w[l],
                            w_out[l])
        h = h + m
        h = h + 0.5 * _swiglu(_rmsnorm(h, ln_ffb_w[l]), ffb_w_gate[l], ffb_w_up[l], ffb_w_down[l])
        conv_new.append(c)
        ssm_new.append(s)
        hg_new.append(g)
    return _rmsnorm(h, ln_f_w), jnp.stack(conv_new), jnp.stack(ssm_new), jnp.stack(hg_new)


def setup_inputs(seed: int = 0) -> dict:
    key = jax.random.key(seed)
    ks = jax.random.split(key, 32)
    f32 = jnp.float32

    def nrm(k, shape, s):
        return s * jax.random.normal(k, shape, f32)

    dt0 = jnp.exp(jax.random.uniform(ks[12], (DEPTH, SSM_HEADS), f32, math.log(1e-3), math.log(1e-1)))
    return {
        "x_prompt": nrm(ks[0], (BATCH, SEQ, D_MODEL), 1.0),
        "x_sample": nrm(ks[1], (DEC_BATCH, DEC_SEQ, D_MODEL), 1.0),
        "state_conv": nrm(ks[2], (DEPTH, DEC_BATCH, CONV_WIDTH - 1, CONV_CH), 1.0),
        "state_ssm": nrm(ks[3], (DEPTH, DEC_BATCH, SSM_HEADS, SSM_HEADDIM, SSM_STATE), 0.5),
        "state_hgrn": nrm(ks[4], (DEPTH, DEC_BATCH, HG_HEADS, HG_EXPAND, HG_VDIM), 0.5),
        "meta_tokens": nrm(ks[5], (N_META, D_MODEL), 1.0),
        "ln_ffa_w": 1.0 + nrm(ks[6], (DEPTH, D_MODEL), 0.01),
        "ffa_w_gate": nrm(ks[7], (DEPTH, D_MODEL, D_FF), D_MODEL ** -0.5),
        "ffa_w_up": nrm(ks[8], (DEPTH, D_MODEL, D_FF), D_MODEL ** -0.5),
        "ffa_w_down": nrm(ks[9], (DEPTH, D_FF, D_MODEL), D_FF ** -0.5),
        "ln_mix_w": 1.0 + nrm(ks[10], (DEPTH, D_MODEL), 0.01),
        "w_in": nrm(ks[11], (DEPTH, D_MODEL, IN_COLS), D_MODEL ** -0.5),
        "conv_w": nrm(ks[13], (DEPTH, CONV_WIDTH, CONV_CH), CONV_WIDTH ** -0.5),
        "conv_b": nrm(ks[14], (DEPTH, CONV_CH), 0.01),
        "dt_bias": dt0 + jnp.log(-jnp.expm1(-dt0)),
        "a_log": jnp.log(jax.random.uniform(ks[15], (DEPTH, SSM_HEADS), f32, 1.0, 16.0)),
        "d_skip": 1.0 + nrm(ks[16], (DEPTH, SSM_HEADS), 0.01),
        "ssm_norm_w": 1.0 + nrm(ks[17], (DEPTH, SSM_WIDTH), 0.01),
        "hg_lb_raw": nrm(ks[18], (DEPTH, HG_WIDTH), 0.1),
        "hg_norm_w": 1.0 + nrm(ks[19], (DEPTH, HG_WIDTH), 0.01),
        "w_out": nrm(ks[20], (DEPTH, MIX_WIDTH, D_MODEL), MIX_WIDTH ** -0.5),
        "ln_ffb_w": 1.0 + nrm(ks[21], (DEPTH, D_MODEL), 0.01),
        "ffb_w_gate": nrm(ks[22], (DEPTH, D_MODEL, D_FF), D_MODEL ** -0.5),
        "ffb_w_up": nrm(ks[23], (DEPTH, D_MODEL, D_FF), D_MODEL ** -0.5),
        "ffb_w_down": nrm(ks[24], (DEPTH, D_FF, D_MODEL), D_FF ** -0.5),
        "ln_f_w": 1.0 + nrm(ks[25], (D_MODEL,), 0.01),
    }


def reference(x_prompt, x_sample, state_conv, state_ssm, state_hgrn, meta_tokens, ln_ffa_w, ffa_w_gate,
              ffa_w_up, ffa_w_down, ln_mix_w, w_in, conv_w, conv_b, dt_bias, a_log, d_skip, ssm_norm_w,
              hg_lb_raw, hg_norm_w, w_out, ln_ffb_w, ffb_w_gate, ffb_w_up, ffb_w_down, ln_f_w):
    sm = jax.nn.softmax(hg_lb_raw.astype(jnp.float32), axis=0)
    hg_lb = jnp.clip(jnp.cumsum(sm, axis=0) - sm[0], 0.0, 1.0 - 1e-6)
    params = (ln_ffa_w, ffa_w_gate, ffa_w_up, ffa_w_down, ln_mix_w, w_in, conv_w, conv_b, dt_bias, a_log,
              d_skip, ssm_norm_w, hg_lb, hg_norm_w, w_out, ln_ffb_w, ffb_w_gate, ffb_w_up, ffb_w_down, ln_f_w)

    b = x_prompt.shape[0]
    dt_ = x_prompt.dtype
    meta = jnp.broadcast_to(meta_tokens.astype(dt_)[None], (b, N_META, D_MODEL))
    h0 = jnp.concatenate([meta, x_prompt], axis=1)
    zc = jnp.zeros((DEPTH, b, CONV_WIDTH - 1, CONV_CH), dt_)
    zs = jnp.zeros((DEPTH, b, SSM_HEADS, SSM_HEADDIM, SSM_STATE), dt_)
    zg = jnp.zeros((DEPTH, b, HG_HEADS, HG_EXPAND, HG_VDIM), dt_)
    hp, conv_p, ssm_p, hg_p = _trunk(h0, zc, zs, zg, params)
    y_prompt = hp[:, N_META:]

    y_sample, conv_s, ssm_s, hg_s = _trunk(x_sample, state_conv, state_ssm, state_hgrn, params)
    return (y_prompt, y_sample, conv_p, ssm_p, hg_p, conv_s, ssm_s, hg_s)
```

```python
import contextlib
import numpy as np
import concourse.bass as bass
import concourse.mybir as mybir
from concourse.bass_utils import run_bass_kernel_spmd

F32 = mybir.dt.float32
BF16 = mybir.dt.bfloat16
AF = mybir.ActivationFunctionType
ALU = mybir.AluOpType

D = 1024
NCH = 8
DFF = 2816
NF = 22
INC = 3592
EPS = 1e-6
NEGBIG = -30000.0
PV_LNW = 0
PV_LNF = 48
PV_CW = 56
PV_CB = 120
PV_DSK = 136
PV_SNW = 144
PV_HNW = 152
PV_HLB = 160
NPV = 168
C_U = 0
C_NEGU = 64
C_NBI = 128
C_SLT = 192
C_ONES = 256
C_SLM = 384
NC64 = 896


class Tl:
    def __init__(self, name, t):
        self.name = name
        self.t = t
        self.last_w = None
        self.readers = {}

    def __getitem__(self, k):
        return self.t[k]


class Op:
    __slots__ = ("eng", "fn", "deps", "key", "tok", "sig", "idx")


class Sched:
    ENGS = ("pe", "act", "dve", "pool", "sp")

    def __init__(self):
        self.streams = {e: [] for e in self.ENGS}
        self.n = 0

    def add(self, eng, fn, reads=(), writes=(), key=None):
        op = Op()
        op.eng, op.fn, op.key, op.sig, op.tok = eng, fn, key, False, None
        op.idx = self.n
        self.n += 1
        deps = set()
        for t in reads:
            if t.last_w is not None:
                deps.add(t.last_w)
        for t in writes:
            if t.last_w is not None:
                deps.add(t.last_w)
            for r in t.readers.values():
                deps.add(r)
        op.deps = deps
        rk = (eng, key)
        for t in reads:
            t.readers[rk] = op
        for t in writes:
            t.last_w = op
            t.readers = {}
        self.streams[eng].append(op)
        return op

    def finalize(self):
        for e in self.ENGS:
            for op in self.streams[e]:
                for d in op.deps:
                    d.sig = True
        dcnt = {}
        for e in self.ENGS:
            c = 0
            for op in self.streams[e]:
                if op.key is None:
                    if op.sig:
                        c += 1
                        op.tok = (("eng", e), c)
                else:
                    dcnt[op.key] = dcnt.get(op.key, 0) + 16
                    op.tok = (("dma", op.key), dcnt[op.key])
        return sorted(dcnt.keys(), key=str)

    def run_stream(self, e, engobj, semof):
        seen = {}
        for op in self.streams[e]:
            need = {}
            for d in op.deps:
                if d.key is None and d.eng == "pe" and e == "pe" and op.key is None:
                    continue
                sk, v = d.tok
                if seen.get(sk, 0) >= v:
                    continue
                if need.get(sk, 0) < v:
                    need[sk] = v
            for sk, v in need.items():
                engobj.wait_ge(semof(sk), v)
                seen[sk] = v
            if op.fn is not None:
                ins = op.fn(engobj)
                if op.key is not None:
                    ins.then_inc(semof(("dma", op.key)), 16)
                elif op.sig:
                    ins.then_inc(semof(("eng", e)), 1)


def interleave(*gens):
    gens = list(gens)
    while gens:
        for g in list(gens):
            try:
                next(g)
            except StopIteration:
                gens.remove(g)


def build(cfg):
    NSEG = cfg.get("nseg", 8)
    DEPTH = cfg.get("depth", 2)
    DO_MIX = cfg.get("mixer", True)
    DO_FFN = cfg.get("ffn", True)
    NPTOK = NSEG * 512
    nc = bass.Bass("TRN2", target_bir_lowering=False)
    S = Sched()
    es = contextlib.ExitStack()

    def din(name, shape, dt=F32):
        return nc.dram_tensor(name, list(shape), dt, kind="ExternalInput").ap()

    def dout(name, shape, dt=F32):
        return nc.dram_tensor(name, list(shape), dt, kind="ExternalOutput").ap()

    xp = din("xp", [NPTOK, D])
    xs = din("xs", [64, D])
    meta = din("meta", [16, D])
    sconv = din("sconv", [2, 3, D])
    sssm = din("sssm", [2, 512, 128])
    shg = din("shg", [2, 4, 128, 128])
    pvec_d = din("pvec", [128, NPV])
    tokp_d = din("tokp", [128, 32])
    ident_d = din("ident", [128, 128])
    c64_d = din("c64", [64, NC64])
    rmask_d = din("rmask", [128, 512])
    wts = {}
    for nm, shp in (("ffa_w_gate", [2, D, DFF]), ("ffa_w_up", [2, D, DFF]), ("ffa_w_down", [2, DFF, D]),
                    ("w_in", [2, D, INC]), ("w_out", [2, D, D]),
                    ("ffb_w_gate", [2, D, DFF]), ("ffb_w_up", [2, D, DFF]), ("ffb_w_down", [2, DFF, D])):
        wts[nm] = din(nm, shp)
    yp = dout("yp", [NPTOK, D])
    ys = dout("ys", [64, D])
    o_conv = [dout("conv_p", [2, 3, D]), dout("conv_s", [2, 3, D])]
    o_ssm = [dout("ssm_p", [2, 512, 128]), dout("ssm_s", [2, 512, 128])]
    o_hg = [dout("hg_p", [2, 4, 128, 128]), dout("hg_s", [2, 4, 128, 128])]

    def sb(name, shape, dt=F32):
        return es.enter_context(nc.sbuf_tensor("s_" + name, list(shape), dt))

    def T(name, shape, dt=F32):
        return Tl(name, sb(name, shape, dt))

    WMAX = 1104
    h = [Tl("h%d" % c, None) for c in range(NCH)]
    h_t = sb("h", [128, NCH, WMAX])
    for c in range(NCH):
        h[c].t = h_t[:, c, :]
    NST, NBF = 4, 4
    wst = [T("wst%d" % i, [128, 1024]) for i in range(NST)]
    wbf = [T("wbf%d" % i, [128, 1024], BF16) for i in range(NBF)]
    ident = T("ident", [128, 128])
    identb = T("identb", [128, 128], BF16)
    onesb = T("onesb", [128, 128], BF16)
    c64 = T("c64", [64, NC64])
    rmask = T("rmask", [128, 512])
    pvec = T("pvec", [128, NPV])
    tokp = T("tokp", [128, 32])
    abc = T("abc", [128, 16])
    lbt = T("lbt", [128, 8])
    omlb = T("omlb", [128, 8])
    wdt = T("wdt", [128, 2 * 8 * 8], BF16)
    wdt_f = T("wdt_f", [128, 2 * 8 * 8])
    io = [T("io%d" % i, [128, D]) for i in range(2)]
    ST = [[T("ST%d%d" % (l, q), [128, 512]) for q in range(2)] for l in range(2)]
    SG = [[T("SG%d%d" % (l, q), [128, 512]) for q in range(2)] for l in range(2)]
    TAILT = [[sb("TL%d%d" % (l, q), [128, 8, 3]) for q in range(2)] for l in range(2)]
    TAIL = [[[Tl("TL%d%d_%d" % (l, q, c), TAILT[l][q][:, c, :]) for c in range(8)] for q in range(2)] for l in range(2)]
    STb2 = [T("STb%d" % i, [128, 512], BF16) for i in range(2)]
    SGb2 = [T("SGb%d" % i, [128, 512], BF16) for i in range(2)]
    scr = T("scr", [128, 8])

    rem = nc.sbuf_bytes_remaining
    ARENA_F32 = ((rem - 2048) // 4) // 64 * 64
    if cfg.get("verbose"):
        print("sbuf remaining for arena", rem, "arena f32", ARENA_F32)
    arena = sb("arena", [128, ARENA_F32])
    apos = [0]

    def carve(shape, dt=F32, reset=False):
        if reset:
            apos[0] = 0
        n = int(np.prod(shape[1:]))
        nf = n if dt == F32 else (n + 1) // 2
        nf = (nf + 15) // 16 * 16
        a0 = apos[0]
        apos[0] += nf
        assert apos[0] <= ARENA_F32, ("arena overflow", apos[0], ARENA_F32)
        v = arena[:, a0:a0 + nf]
        if dt != F32:
            v = v.bitcast(dt)
        v = v[:, 0:n]
        if len(shape) == 3:
            v = v.rearrange("p (a b) -> p a b", b=shape[2])
        if shape[0] != 128:
            v = v[0:shape[0]]
        return v

    def AT(name, shape, dt=F32, reset=False):
        return Tl(name, carve(shape, dt, reset))

    xn = AT("xn", [128, NCH, WMAX], BF16, reset=True)
    actT = [None] * NF
    act_all = carve([128, NF, WMAX], BF16)
    for f in range(NF):
        actT[f] = Tl("act%d" % f, act_all[:, f, :])
    sqt = [AT("sqt%d" % i, [128, 512], BF16) for i in range(2)]
    lnv = AT("lnv", [128, 512])
    rstd = AT("rstd", [128, 512])
    sg = [AT("sg%d" % i, [128, 512]) for i in range(2)]
    ffn_end = apos[0]
    hn = AT("hn", [128, NCH, 512], BF16, reset=True)
    msq = [AT("msq%d" % i, [128, 512], BF16) for i in range(2)]
    mlnv = AT("mlnv", [128, 512])
    mrstd = AT("mrstd", [128, 512])
    xpre = [AT("xpre%d" % i, [128, 528]) for i in range(2)]
    cacc = [AT("cacc%d" % i, [128, 512]) for i in range(2)]
    xact = AT("xact", [128, 4, 512])
    BT = AT("BT", [128, 2, 512], BF16)
    CT = AT("CT", [128, 2, 512], BF16)
    zs = AT("zs", [128, 4, 512], BF16)
    yT = AT("yT", [128, 4, 512])
    cum = AT("cum", [128, 4, 512])
    qt = AT("qt", [128, 4, 512], BF16)
    kt = AT("kt", [128, 4, 512], BF16)
    qe = AT("qe", [128, 4, 512], BF16)
    kdT = AT("kdT", [128, 4, 512], BF16)
    gs = AT("gs", [128, 4, 512], BF16)
    vtok = AT("vtok", [64, 8, 512], BF16)
    oT = yT
    tA = AT("tA", [128, 512])
    tB = AT("tB", [128, 512])
    tC = AT("tC", [128, 512])
    elast = AT("elast", [128, 4, 8])
    dtt = AT("dtt", [64, 80])
    lat = AT("lat", [64, 80])
    dtw = AT("dtw", [64, 8])
    sml = [AT("sml%d" % i, [128, 24]) for i in range(2)]
    laU = AT("laU", [128, 512])
    lab = AT("lab", [128, 512])
    Lm = AT("Lm", [128, 512])
    Mm = [AT("Mm%d" % i, [64, 512], BF16) for i in range(2)]
    xdt = [AT("xdt%d" % i, [64, 512], BF16) for i in range(2)]
    wxdt = [AT("wxdt%d" % i, [64, 512], BF16) for i in range(2)]
    btok = [AT("btok%d" % i, [64, 256], BF16) for i in range(2)]
    yraw = AT("yraw", [64, 512])
    ytmp = AT("ytmp", [64, 512])
    kdtok = [AT("kdtok%d" % i, [64, 512], BF16) for i in range(2)]
    attm = [AT("attm%d" % i, [64, 256], BF16) for i in range(2)]
    if cfg.get("verbose"):
        print("arena use: ffn", ffn_end, "mixer", apos[0], "of", ARENA_F32)
    ARENA_TILES_F = [xn] + actT + sqt + [lnv, rstd] + sg
    ARENA_TILES_M = ([hn] + msq + [mlnv, mrstd] + xpre + cacc + [xact, BT, CT, zs, yT, cum, qt, kt, qe, kdT, gs, vtok,
                                                              tA, tB, tC, elast, dtt, lat, dtw, laU, lab, Lm, yraw, ytmp]
                     + sml + Mm + xdt + wxdt + btok + kdtok + attm)
    PB = [Tl("pb%d" % i, es.enter_context(nc.psum_tensor("pb%d" % i, [128, 512], F32))) for i in range(8)]

    def pbf(i):
        return PB[i].t[:, :].bitcast(BF16)

    def dma(out_ap, in_ap, reads, writes, key, eng="sp"):
        def fn(e, o=out_ap, i=in_ap):
            return e.dma_start(out=o, in_=i)
        S.add(eng, fn, reads=reads, writes=writes, key=key)

    def act(out_ap, in_ap, func, reads, writes, bias=None, scale=None):
        def fn(e, o=out_ap, i=in_ap, f=func, b=bias, s=scale):
            kw = {}
            if b is not None:
                kw["bias"] = b
            if s is not None:
                kw["scale"] = s
            return e.activation(o, i, f, **kw)
        S.add("act", fn, reads=reads, writes=writes)

    def tt(out_ap, a, b, op, reads, writes, eng="dve"):
        def fn(e, o=out_ap, a=a, b=b, op=op):
            return e.tensor_tensor(o, a, b, op)
        S.add(eng, fn, reads=reads, writes=writes)

    def ts(out_ap, a, s1, s2, op0, op1, reads, writes, eng="dve"):
        def fn(e, o=out_ap, a=a, s1=s1, s2=s2, op0=op0, op1=op1):
            if op1 is None:
                return e.tensor_scalar(o, a, s1, None, op0)
            return e.tensor_scalar(o, a, s1, s2, op0, op1)
        S.add(eng, fn, reads=reads, writes=writes)

    def stt(out_ap, a, sc, b, op0, op1, reads, writes):
        def fn(e, o=out_ap, a=a, sc=sc, b=b, op0=op0, op1=op1):
            return e.scalar_tensor_tensor(o, a, sc, b, op0, op1)
        S.add("dve", fn, reads=reads, writes=writes)

    def cp(out_ap, in_ap, reads, writes, eng="dve"):
        def fn(e, o=out_ap, i=in_ap):
            return e.tensor_copy(o, i)
        S.add(eng, fn, reads=reads, writes=writes)

    def memset(tile, ap, val, eng="dve"):
        def fn(e, a=ap, v=val):
            return e.memset(a, v)
        S.add(eng, fn, reads=[], writes=[tile])

    def mm(out_ap, pairs, reads, writes):
        def fn(e, o=out_ap, pairs=pairs):
            ins = None
            n = len(pairs)
            for i, (l, r) in enumerate(pairs):
                ins = e.matmul(o, l, r, start=(i == 0), stop=(i == n - 1))
            return ins
        S.add("pe", fn, reads=reads, writes=writes)

    def mm_multi(groups, reads, writes):
        def fn(e, groups=groups):
            ins = None
            for (o, pairs) in groups:
                n = len(pairs)
                for i, (l, r) in enumerate(pairs):
                    ins = e.matmul(o, l, r, start=(i == 0), stop=(i == n - 1))
            return ins
        S.add("pe", fn, reads=reads, writes=writes)

    def tr_multi(items, reads, writes):
        def fn(e, items=items):
            ins = None
            for (o, i, idn) in items:
                ins = e.transpose(o, i, idn)
            return ins
        S.add("pe", fn, reads=reads, writes=writes)

    def pcol(base, idx):
        return pvec.t[:, base + idx: base + idx + 1]

    DN_PARTS = [(0, 6), (6, 12), (12, 17), (17, 22)]

    class WS:
        items = []
        nload = 0
        ncast = 0
        nuse = 0
        cast_eng = "act"

    def w_panel(nm, l, kind, j):
        w = wts[nm]
        if kind == "gu":
            return (w[l].rearrange("(c p) f -> p c f", p=128)[:, :, j * 128:(j + 1) * 128], 8, 128)
        if kind == "dn":
            dc, part = j
            f0, f1 = DN_PARTS[part]
            return (w[l].rearrange("(c p) d -> p c d", p=128)[:, f0:f1, dc * 128:(dc + 1) * 128], f1 - f0, 128)
        if kind == "in":
            return (w[l].rearrange("(c p) f -> p c f", p=128)[:, :, j:j + 128], 8, 128)
        if kind == "out":
            return (w[l].rearrange("(c p) f -> p c f", p=128)[:, :, j * 128:(j + 1) * 128], 8, 128)
        raise ValueError

    def w_issue_load():
        k = WS.nload
        if k >= len(WS.items):
            return
        ap, R, C = WS.items[k]
        st = wst[k % NST]
        dma(st.t[:, 0:R * C].rearrange("p (r c) -> p r c", c=C), ap, [], [st], key="wst%d" % (k % NST))
        WS.nload += 1

    def w_issue_cast():
        k = WS.ncast
        if k >= len(WS.items):
            return
        ap, R, C = WS.items[k]
        st = wst[k % NST]
        bf = wbf[k % NBF]
        if WS.cast_eng == "act":
            act(bf.t[:, 0:R * C], st.t[:, 0:R * C], AF.Copy, [st], [bf])
        else:
            cp(bf.t[:, 0:R * C], st.t[:, 0:R * C], [st], [bf], eng=WS.cast_eng)
        WS.ncast += 1

    def w_get():
        k = WS.nuse
        CA = cfg.get("cast_ahead", 2)
        t_load = min(len(WS.items), k + CA + 4)
        t_cast = min(len(WS.items), k + CA + 1)
        while True:
            if WS.nload < t_load and WS.nload < WS.ncast + NST:
                w_issue_load()
            elif WS.ncast < t_cast and WS.ncast < WS.nload:
                w_issue_cast()
            else:
                break
        ap, R, C = WS.items[k]
        WS.nuse += 1
        bf = wbf[k % NBF]
        return bf, bf.t[:, 0:R * C].rearrange("p (r c) -> p r c", c=C)

    NSUP = NSEG // 2
    def sup_layout(s):
        if s == 0:
            return 1104, [(0, 80), (80, 592), (592, 1104)]
        return 1024, [(0, 512), (512, 1024)]

    IN_OFFS = ([c * 128 for c in range(4)] + [512 + c * 128 for c in range(8)]
               + [base + hd * 128 for pair in ((0, 1), (2, 3)) for base in (2056, 1544, 2568, 3080) for hd in pair])
    def mixer_passes(s):
        if s == 0:
            return [(0, 80, [(1, 0, 64, 64), (0, 64, 80, 16)]), (80, 592, [(0, 0, 512, 64)]), (592, 1104, [(0, 0, 512, 64)])]
        return [(0, 512, [(0, 0, 512, 64)]), (512, 1024, [(0, 0, 512, 64)])]

    for s in range(NSUP):
        for l in range(DEPTH):
            if DO_FFN:
                for j in range(22):
                    WS.items.append(w_panel("ffa_w_gate", l, "gu", j))
                    WS.items.append(w_panel("ffa_w_up", l, "gu", j))
                for dc in range(8):
                    for part in range(4):
                        WS.items.append(w_panel("ffa_w_down", l, "dn", (dc, part)))
            if DO_MIX:
                for _ in mixer_passes(s):
                    for off in IN_OFFS:
                        WS.items.append(w_panel("w_in", l, "in", off))
                    for j in range(8):
                        WS.items.append(w_panel("w_out", l, "out", j))
            if DO_FFN:
                for j in range(22):
                    WS.items.append(w_panel("ffb_w_gate", l, "gu", j))
                    WS.items.append(w_panel("ffb_w_up", l, "gu", j))
                for dc in range(8):
                    for part in range(4):
                        WS.items.append(w_panel("ffb_w_down", l, "dn", (dc, part)))

    dma(ident.t[:, :], ident_d, [], [ident], key="c0")
    dma(c64.t[:, :], c64_d, [], [c64], key="c1")
    dma(rmask.t[:, :], rmask_d, [], [rmask], key="c2")
    dma(pvec.t[:, :], pvec_d, [], [pvec], key="c3")
    dma(tokp.t[:, :], tokp_d, [], [tokp], key="c4")
    cp(identb.t[:, :], ident.t[:, :], [ident], [identb])
    memset(onesb, onesb.t[:, :], 1.0)
    memset(scr, scr.t[:, :], 0.0)
    for l in range(2):
        dma(wdt_f.t[:, l * 64:(l + 1) * 64].rearrange("p (c k) -> p c k", k=8),
            wts["w_in"][l].rearrange("(c p) f -> p c f", p=128)[:, :, 1536:1544], [], [wdt_f], key="c5")
    cp(wdt.t[:, :], wdt_f.t[:, :], [wdt_f], [wdt])
    act(abc.t[:, :], tokp.t[:, 16:32], AF.Exp, [tokp], [abc])
    ts(abc.t[:, :], abc.t[:, :], -1.0, None, ALU.mult, None, [abc], [abc])
    memset(lbt, lbt.t[:, :], 0.0)
    act(omlb.t[:, 0:8], pvec.t[:, PV_HLB:PV_HLB + 8], AF.Exp, [pvec], [omlb])
    tt(omlb.t[:, 0:4], omlb.t[:, 0:4], omlb.t[:, 4:8], ALU.add, [omlb], [omlb])
    def _recip(e):
        return e.reciprocal(omlb.t[:, 0:4], omlb.t[:, 0:4])
    S.add("dve", _recip, reads=[omlb], writes=[omlb])
    tt(lbt.t[:, 4:8], omlb.t[:, 4:8], omlb.t[:, 0:4], ALU.mult, [omlb, lbt], [lbt])
    ts(lbt.t[:, 4:8], lbt.t[:, 4:8], 1.0 - 1e-6, None, ALU.min, None, [lbt], [lbt])
    ts(omlb.t[:, :], lbt.t[:, :], -1.0, 1.0, ALU.mult, ALU.add, [lbt], [omlb])
    for l in range(2):
        memset(ST[l][0], ST[l][0].t[:, :], 0.0)
        memset(SG[l][0], SG[l][0].t[:, :], 0.0)
        def fms(e, l=l):
            return e.memset(TAILT[l][0][:, :, :], 0.0)
        S.add("dve", fms, reads=[], writes=TAIL[l][0])
    if DO_MIX:
        for l in range(DEPTH):
            dma(SG[l][1].t[:, :].rearrange("p (h v) -> p h v", v=128), shg[l].rearrange("h k v -> k h v"),
                [], [SG[l][1]], key="sg%d" % l)
            t0 = io[0]
            dma(t0.t[:, 0:512].rearrange("p (c n) -> p c n", n=128), sssm[l].rearrange("(c p) n -> p c n", p=128),
                [], [t0], key="io0")
            tr_multi([(PB[0].t[:, c * 128:(c + 1) * 128], t0.t[:, c * 128:(c + 1) * 128], ident.t[:, :]) for c in range(4)],
                     [t0, ident], [PB[0]])
            cp(ST[l][1].t[:, :], PB[0].t[:, :], [PB[0]], [ST[l][1]])
            t1 = io[1]
            dma(t1.t[0:3, :], sconv[l], [], [t1], key="io1")
            tr_multi([(PB[1].t[:, c * 4:c * 4 + 3], t1.t[0:3, c * 128:(c + 1) * 128], ident.t[0:3, 0:3]) for c in range(8)],
                     [t1, ident], [PB[1]])
            cp(TAILT[l][1][:, :, :], PB[1].t[:, 0:32].rearrange("p (c k) -> p c k", k=4)[:, :, 0:3], [PB[1]], TAIL[l][1])

    def rmsnorm(src_aps, src_tiles, dst_tile, dst_aps, wbase, n, sq2, lv, rs, pbank, out_f32_tile=None):
        for c in range(NCH):
            q = sq2[c % 2]
            act(q.t[:, 0:n], src_aps[c], AF.Square, [src_tiles[c]], [q])
            def fn(e, c=c, q=q):
                return e.matmul(pbank.t[:, 0:n], onesb.t[:, :], q.t[:, 0:n], start=(c == 0), stop=(c == NCH - 1))
            S.add("pe", fn, reads=[onesb, q], writes=[pbank])
        act(lv.t[:, 0:n], pbank.t[:, 0:n], AF.Ln, [pbank], [lv], bias=EPS, scale=1.0 / D)
        act(rs.t[:, 0:n], lv.t[:, 0:n], AF.Exp, [lv], [rs], scale=-0.5)
        for c in range(NCH):
            stt(dst_aps[c], src_aps[c], pcol(wbase, c), rs.t[:, 0:n], ALU.mult, ALU.mult,
                [src_tiles[c], pvec, rs], [dst_tile])

    def arena_switch(from_tiles=None, to_tiles=None):
        def fn(e):
            return e.memset(scr.t[:, 0:1], 0.0)
        S.add("dve", fn, reads=[], writes=ARENA_TILES_F + ARENA_TILES_M + [scr])

    def load_tokens(src_ap, ntok, col0, slot):
        t = io[slot]
        dma(t.t[0:ntok, :], src_ap, [], [t], key="io%d" % slot)
        for half in range(2):
            pb = PB[half]
            tr_multi([(pb.t[:, k * 128:k * 128 + ntok], t.t[0:ntok, (half * 4 + k) * 128:(half * 4 + k + 1) * 128],
                       ident.t[0:ntok, 0:ntok]) for k in range(4)], [t, ident], [pb])
            def fn(e, pb=pb, half=half):
                return e.activation(h_t[:, half * 4:half * 4 + 4, col0:col0 + ntok],
                                    pb.t[:, :].rearrange("p (k t) -> p k t", t=128)[:, :, 0:ntok], AF.Copy)
            S.add("act", fn, reads=[pb], writes=[h[half * 4 + k] for k in range(4)])

    def store_tokens(dst_ap, ntok, col0, slot, ytile, yview):
        t = io[slot]
        for half in range(2):
            pb = PB[2 + half]
            tr_multi([(pb.t[0:ntok, k * 128:(k + 1) * 128], yview[:, half * 4 + k, col0:col0 + ntok], ident.t[:, :])
                      for k in range(4)], [ytile, ident], [pb])
            def fn(e, pb=pb, half=half, t=t):
                return e.activation(t.t[0:ntok, half * 512:(half + 1) * 512], pb.t[0:ntok, :], AF.Copy)
            S.add("act", fn, reads=[pb], writes=[t])
        dma(dst_ap, t.t[0:ntok, :], [t], [], key="io%d" % slot)

    def ffn(l, kind, W, nchunks):
        wbase = PV_LNW + (l * 3 + kind) * 8
        for (c0, c1) in nchunks:
            n = c1 - c0
            rmsnorm([h_t[:, c, c0:c1] for c in range(NCH)], h, xn, [xn.t[:, c, c0:c1] for c in range(NCH)],
                    wbase, n, sqt, lnv, rstd, PB[6])
        it = 0
        for f in range(NF):
            wgT, wg = w_get()
            wuT, wu = w_get()
            for (c0, c1) in nchunks:
                n = c1 - c0
                pg = PB[it % 2]
                pu = PB[2 + it % 2]
                sgt = sg[it % 2]
                it += 1
                mm(pg.t[:, 0:n], [(wg[:, dc, :], xn.t[:, dc, c0:c1]) for dc in range(NCH)], [wgT, xn], [pg])
                mm(pu.t[:, 0:n], [(wu[:, dc, :], xn.t[:, dc, c0:c1]) for dc in range(NCH)], [wuT, xn], [pu])
                act(sgt.t[:, 0:n], pg.t[:, 0:n], AF.Silu, [pg], [sgt])
                tt(actT[f].t[:, c0:c1], sgt.t[:, 0:n], pu.t[:, 0:n], ALU.mult, [sgt, pu], [actT[f]])
        pybanks = [PB[4], PB[5], PB[7]]
        for dc in range(NCH):
            for pi, (f0, f1) in enumerate(DN_PARTS):
                wT, w = w_get()
                for ci, (c0, c1) in enumerate(nchunks):
                    n = c1 - c0
                    py = pybanks[ci]
                    def fn(e, w=w, f0=f0, f1=f1, c0=c0, c1=c1, n=n, py=py):
                        ins = None
                        for f in range(f0, f1):
                            ins = e.matmul(py.t[:, 0:n], w[:, f - f0, :], actT[f].t[:, c0:c1], start=(f == 0), stop=(f == NF - 1))
                        return ins
                    S.add("pe", fn, reads=[wT] + actT[f0:f1], writes=[py])
            for ci, (c0, c1) in enumerate(nchunks):
                n = c1 - c0
                py = pybanks[ci]
                stt(h_t[:, dc, c0:c1], py.t[:, 0:n], 0.5, h_t[:, dc, c0:c1], ALU.mult, ALU.add, [py, h[dc]], [h[dc]])

    def psum_split(bank, subs):
        def fn(e):
            return e.memset(scr.t[:, 1:2], 0.0)
        S.add("dve", fn, reads=[], writes=[bank] + subs + [scr])

    def mixer_pass(l, pc0, pc1, units):
        PW = pc1 - pc0
        lnb = PV_LNW + (l * 3 + 1) * 8
        rmsnorm([h_t[:, c, pc0:pc1] for c in range(NCH)], h, hn, [hn.t[:, c, 0:PW] for c in range(NCH)],
                lnb, PW, msq, mlnv, mrstd, PB[7])
        blocks = []
        ublocks = []
        for (seq, u0, u1, Q) in units:
            ub = []
            for b in range((u1 - u0) // Q):
                blk = (seq, u0 + b * Q, Q, len(blocks))
                blocks.append(blk)
                ub.append(blk)
            ublocks.append(ub)
        nblk = len(blocks)
        rot = [0]

        NBK = [0, 1, 2, 6, 7, 3]

        def nb():
            rot[0] += 1
            return PB[NBK[rot[0] % 6]]

        def inproj(w, wT, pb):
            mm(pb.t[:, 0:PW], [(w[:, dc, :], hn.t[:, dc, 0:PW]) for dc in range(NCH)], [wT, hn], [pb])

        for c in range(4):
            wT, w = w_get()
            pb = nb()
            inproj(w, wT, pb)
            act(zs.t[:, c, 0:PW], pb.t[:, 0:PW], AF.Silu, [pb], [zs])
        def conv_gen(c):
            wT, w = w_get()
            pb = nb()
            xb = xpre[c % 2]
            ca = cacc[c % 2]
            inproj(w, wT, pb)
            yield
            for ui, (seq, u0, u1, Q) in enumerate(units):
                n = u1 - u0
                bo = u0 + 3 * (ui + 1)
                tl = TAIL[l][seq][c]
                act(xb.t[:, bo - 3:bo], tl.t, AF.Copy, [tl], [xb])
                act(xb.t[:, bo:bo + n], pb.t[:, u0:u1], AF.Copy, [pb], [xb])
                yield
                act(ca.t[:, u0:u1], pb.t[:, u0:u1], AF.Identity, [pb, pvec], [ca],
                    bias=pcol(PV_CB, l * 8 + c), scale=pcol(PV_CW, (l * 4 + 3) * 8 + c))
                yield
                for k in range(3):
                    stt(ca.t[:, u0:u1], xb.t[:, bo - 3 + k:bo - 3 + k + n], pcol(PV_CW, (l * 4 + k) * 8 + c),
                        ca.t[:, u0:u1], ALU.mult, ALU.add, [xb, pvec, ca], [ca])
                    yield
                act(tl.t, xb.t[:, bo + n - 3:bo + n], AF.Copy, [xb], [tl])
            if c < 4:
                act(xact.t[:, c, 0:PW], ca.t[:, 0:PW], AF.Silu, [ca], [xact])
            elif c < 6:
                act(BT.t[:, c - 4, 0:PW], ca.t[:, 0:PW], AF.Silu, [ca], [BT])
            else:
                act(CT.t[:, c - 6, 0:PW], ca.t[:, 0:PW], AF.Silu, [ca], [CT])
            yield

        for c in range(0, 8, 2):
            interleave(conv_gen(c), conv_gen(c + 1))
        pdt = PB[3]
        groups = []
        for (seq, cb, Q, bi) in blocks:
            groups.append((pdt.t[0:Q, bi * 8:(bi + 1) * 8],
                           [(hn.t[:, dc, cb:cb + Q], wdt.t[:, (l * 8 + dc) * 8:(l * 8 + dc + 1) * 8]) for dc in range(NCH)]))
        mm_multi(groups, [hn, wdt], [pdt])
        bi0 = 0
        for (seq, u0, u1, Q) in units:
            nbk = (u1 - u0) // Q
            sl = slice(bi0 * 8, (bi0 + nbk) * 8)
            tt(dtt.t[0:Q, sl].rearrange("p (b k) -> p b k", k=8), pdt.t[0:Q, sl].rearrange("p (b k) -> p b k", k=8),
               tokp.t[0:Q, l * 8:(l + 1) * 8].unsqueeze(1).broadcast_to([Q, nbk, 8]), ALU.add, [pdt, tokp], [dtt])
            act(dtt.t[0:Q, sl], dtt.t[0:Q, sl], AF.Exp, [dtt], [dtt])
            act(dtt.t[0:Q, sl], dtt.t[0:Q, sl], AF.Ln, [dtt], [dtt], bias=1.0)
            tt(lat.t[0:Q, sl].rearrange("p (b k) -> p b k", k=8), dtt.t[0:Q, sl].rearrange("p (b k) -> p b k", k=8),
               abc.t[0:Q, l * 8:(l + 1) * 8].unsqueeze(1).broadcast_to([Q, nbk, 8]), ALU.mult, [dtt, abc], [lat])
            bi0 += nbk
        GT = [(tA, tB, tC), (laU, lab, Lm)]

        def head_gen(hd):
            gA, gB, gC = GT[hd % 2]
            wT, w = w_get()
            pb = nb()
            inproj(w, wT, pb)
            yield
            lbc = lbt.t[:, l * 4 + hd:l * 4 + hd + 1]
            omc = omlb.t[:, l * 4 + hd:l * 4 + hd + 1]
            act(gA.t[:, 0:PW], pb.t[:, 0:PW], AF.Exp, [pb], [gA], scale=-1.0)
            yield
            ts(gB.t[:, 0:PW], gA.t[:, 0:PW], 1.0, None, ALU.add, None, [gA], [gB])
            yield
            def frec(e_, PW=PW, gC=gC, gB=gB):
                return e_.reciprocal(gC.t[:, 0:PW], gB.t[:, 0:PW])
            S.add("dve", frec, reads=[gB], writes=[gC])
            yield
            stt(gC.t[:, 0:PW], gA.t[:, 0:PW], omc, gC.t[:, 0:PW], ALU.mult, ALU.mult, [gA, omlb, gC], [gC])
            yield
            act(gB.t[:, 0:PW], gB.t[:, 0:PW], AF.Ln, [gB], [gB])
            yield
            act(gA.t[:, 0:PW], gA.t[:, 0:PW], AF.Ln, [gA, lbt], [gA], bias=1.0, scale=lbc)
            yield
            tt(gA.t[:, 0:PW], gA.t[:, 0:PW], gB.t[:, 0:PW], ALU.subtract, [gA, gB], [gA])
            yield
            for (seq, u0, u1, Q) in units:
                def fscan(e_, u0=u0, u1=u1, hd=hd, gA=gA):
                    return e_.tensor_tensor_scan(cum.t[:, hd, u0:u1], rmask.t[:, 0:u1 - u0], gA.t[:, u0:u1], 0.0,
                                                 ALU.mult, ALU.add)
                S.add("dve", fscan, reads=[rmask, gA], writes=[cum])
            for (seq, u0, u1, Q) in units:
                nbk = (u1 - u0) // Q
                cv = cum.t[:, hd, u0:u1].rearrange("p (b q) -> p b q", q=Q)
                tv = gB.t[:, u0:u1].rearrange("p (b q) -> p b q", q=Q)
                tt(tv, cv, cv[:, :, Q // 2 - 1:Q // 2].broadcast_to([128, nbk, Q]), ALU.subtract, [cum], [gB])
            act(gB.t[:, 0:PW], gB.t[:, 0:PW], AF.Exp, [gB], [gB], scale=-1.0)
            yield
            tt(kt.t[:, hd, 0:PW], gC.t[:, 0:PW], gB.t[:, 0:PW], ALU.mult, [gC, gB], [kt])
            yield
            for (seq, u0, u1, Q) in units:
                nbk = (u1 - u0) // Q
                cv = cum.t[:, hd, u0:u1].rearrange("p (b q) -> p b q", q=Q)
                tv = gB.t[:, u0:u1].rearrange("p (b q) -> p b q", q=Q)
                tt(tv, cv, cv[:, :, Q - 1:Q].broadcast_to([128, nbk, Q]), ALU.subtract, [cum], [gB])
            act(gB.t[:, 0:PW], gB.t[:, 0:PW], AF.Exp, [gB], [gB], scale=-1.0)
            yield
            tt(kdT.t[:, hd, 0:PW], gC.t[:, 0:PW], gB.t[:, 0:PW], ALU.mult, [gC, gB], [kdT])
            yield
            bi0 = 0
            for (seq, u0, u1, Q) in units:
                nbk = (u1 - u0) // Q
                act(elast.t[:, hd, bi0:bi0 + nbk], cum.t[:, hd, u0:u1].rearrange("p (b q) -> p b q", q=Q)[:, :, Q - 1],
                    AF.Exp, [cum], [elast])
                bi0 += nbk
            for (seq, u0, u1, Q) in units:
                nbk = (u1 - u0) // Q
                cv = cum.t[:, hd, u0:u1].rearrange("p (b q) -> p b q", q=Q)
                tv = gB.t[:, u0:u1].rearrange("p (b q) -> p b q", q=Q)
                tt(tv, cv, cv[:, :, Q // 2 - 1:Q // 2].broadcast_to([128, nbk, Q]), ALU.subtract, [cum], [gB])
            act(gB.t[:, 0:PW], gB.t[:, 0:PW], AF.Exp, [gB], [gB])
            yield
            act(gC.t[:, 0:PW], cum.t[:, hd, 0:PW], AF.Exp, [cum], [gC])
            yield
            wT, w = w_get()
            pb = nb()
            inproj(w, wT, pb)
            yield
            act(gA.t[:, 0:PW], pb.t[:, 0:PW], AF.Silu, [pb], [gA])
            yield
            tt(qt.t[:, hd, 0:PW], gA.t[:, 0:PW], gB.t[:, 0:PW], ALU.mult, [gA, gB], [qt])
            yield
            tt(qe.t[:, hd, 0:PW], gA.t[:, 0:PW], gC.t[:, 0:PW], ALU.mult, [gA, gC], [qe])
            yield
            wT, w = w_get()
            gi = 0
            for ub in ublocks:
                for g0 in range(0, len(ub), 4):
                    grp = ub[g0:g0 + 4]
                    Q = grp[0][2]
                    pb = PB[4 + (gi + hd) % 2]
                    gi += 1
                    mm_multi([(pb.t[0:Q, j * 128:(j + 1) * 128], [(hn.t[:, dc, cb:cb + Q], w[:, dc, :]) for dc in range(NCH)])
                              for j, (seq, cb, Q_, bi) in enumerate(grp)], [wT, hn], [pb])
                    bi_0 = grp[0][3]
                    ng = len(grp)
                    act(vtok.t[0:Q, bi_0:bi_0 + ng, hd * 128:(hd + 1) * 128],
                        pb.t[0:Q, 0:ng * 128].rearrange("p (b v) -> p b v", v=128), AF.Copy, [pb], [vtok])
            yield
            wT, w = w_get()
            pb = nb()
            inproj(w, wT, pb)
            yield
            act(gs.t[:, hd, 0:PW], pb.t[:, 0:PW], AF.Silu, [pb], [gs])
            yield


        interleave(head_gen(0), head_gen(1))
        interleave(head_gen(2), head_gen(3))

        P1cbm = P1sm = P1bt = PB[1]
        p1 = PB[1].t
        p1bf = pbf(1)
        segb = [PB[0], PB[0]]
        p7bf = pbf(7)
        ssd_cur = [None]

        def ssd_A(blk, k):
            (seq, cb, Q, bi) = blk
            la_b = lat.t[0:Q, bi * 8:(bi + 1) * 8]
            tt(laU.t[0:Q, :].rearrange("p (h i) -> p h i", i=64), la_b.unsqueeze(2).broadcast_to([Q, 8, 64]),
               c64.t[0:Q, C_U:C_U + 64].unsqueeze(1).broadcast_to([Q, 8, 64]), ALU.mult, [lat, c64], [laU], eng="pool")
            cp(lab.t[0:Q, :].rearrange("p (h i) -> p h i", i=64), la_b.unsqueeze(2).broadcast_to([Q, 8, 64]),
               [lat], [lab], eng="pool")
            pseg = segb[k]
            segv = pseg.t[0:Q, :].rearrange("p (h i) -> p h i", i=64)[:, :, 0:Q]
            def fseg(e, Q=Q, pseg=pseg):
                e.matmul(pseg.t[0:Q, :], c64.t[0:Q, C_ONES:C_ONES + Q], laU.t[0:Q, :], start=True, stop=False)
                e.matmul(pseg.t[0:Q, :], c64.t[0:Q, C_NEGU:C_NEGU + Q], lab.t[0:Q, :], start=False, stop=False)
                return e.matmul(pseg.t[0:Q, :], c64.t[0:Q, C_NBI:C_NBI + Q], c64.t[0:Q, C_SLM:C_SLM + 512],
                                start=False, stop=True)
            S.add("pe", fseg, reads=[c64, laU, lab], writes=[pseg])
            def fsm(e, Q=Q, la_b=la_b):
                e.matmul(p1[0:Q, 256:264], c64.t[0:Q, C_U:C_U + Q], la_b, start=True, stop=True)
                e.matmul(p1[0:Q, 264:272], c64.t[0:Q, C_SLT:C_SLT + Q], la_b, start=True, stop=True)
                return e.matmul(p1[:, 272:280], c64.t[0:Q, C_ONES:C_ONES + 128], la_b, start=True, stop=True)
            S.add("pe", fsm, reads=[c64, lat], writes=[P1sm])
            def fcb(e, Q=Q, cb=cb):
                ins = None
                for g in range(2):
                    ins = e.matmul(p1[0:Q, g * 64:g * 64 + Q], BT.t[:, g, cb:cb + Q], CT.t[:, g, cb:cb + Q], start=True, stop=True)
                return ins
            S.add("pe", fcb, reads=[BT, CT], writes=[P1cbm])
            tr_multi([(p7bf[0:Q, g * 128:(g + 1) * 128], BT.t[:, g, cb:cb + Q], identb.t[:, :]) for g in range(2)],
                     [BT, identb], [PB[7]])
            sm = sml[k]
            act(sm.t[0:Q, 0:16], p1[0:Q, 256:272], AF.Exp, [P1sm], [sm])
            act(sm.t[:, 16:24], p1[:, 272:280], AF.Exp, [P1sm], [sm])
            act(Lm.t[0:Q, :].rearrange("p (h i) -> p h i", i=64)[:, :, 0:Q], segv, AF.Exp, [pseg], [Lm])
            M = Mm[k]
            tt(M.t[0:Q, :].rearrange("p (g r i) -> p g r i", r=4, i=64)[:, :, :, 0:Q],
               Lm.t[0:Q, :].rearrange("p (g r i) -> p g r i", r=4, i=64)[:, :, :, 0:Q],
               p1[0:Q, 0:128].rearrange("p (g i) -> p g i", i=64)[:, :, 0:Q].unsqueeze(2).broadcast_to([Q, 2, 4, Q]),
               ALU.mult, [Lm, P1cbm], [M])
            pxt = PB[2]
            tr_multi([(pxt.t[0:Q, c * 128:(c + 1) * 128], xact.t[:, c, cb:cb + Q], ident.t[:, :]) for c in range(4)],
                     [xact, ident], [pxt])
            bk = btok[k]
            act(bk.t[0:Q, :], p7bf[0:Q, 0:256], AF.Copy, [PB[7]], [bk])
            dt_b = dtt.t[0:Q, bi * 8:(bi + 1) * 8]
            tt(dtw.t[0:Q, :], dt_b, sm.t[0:Q, 8:16], ALU.mult, [dtt, sm], [dtw])
            tt(xdt[k].t[0:Q, :].rearrange("p (h d) -> p h d", d=64), pxt.t[0:Q, :].rearrange("p (h d) -> p h d", d=64),
               dt_b.unsqueeze(2).broadcast_to([Q, 8, 64]), ALU.mult, [pxt, dtt], [xdt[k]])
            tt(wxdt[k].t[0:Q, :].rearrange("p (h d) -> p h d", d=64), pxt.t[0:Q, :].rearrange("p (h d) -> p h d", d=64),
               dtw.t[0:Q, :].unsqueeze(2).broadcast_to([Q, 8, 64]), ALU.mult, [pxt, dtw], [wxdt[k]])

        def ssd_Bpe(blk, k):
            (seq, cb, Q, bi) = blk
            Sst = ST[l][seq]
            if ssd_cur[0] != seq:
                act(STb2[(bi + 1) % 2].t[:, :], Sst.t[:, :], AF.Copy, [Sst], [STb2[(bi + 1) % 2]])
                ssd_cur[0] = seq
            STp = STb2[(bi + 1) % 2]
            M = Mm[k]
            pyi, pyo, pup = PB[3], PB[4], PB[5]
            def fup(e, Q=Q, bk=btok[k], wx=wxdt[k]):
                ins = None
                for g in range(2):
                    ins = e.matmul(pup.t[:, g * 256:(g + 1) * 256], bk.t[0:Q, g * 128:(g + 1) * 128],
                                   wx.t[0:Q, g * 256:(g + 1) * 256], start=True, stop=True)
                return ins
            S.add("pe", fup, reads=[btok[k], wxdt[k]], writes=[pup])
            def fyo(e, Q=Q, cb=cb, STp=STp):
                ins = None
                for g in range(2):
                    ins = e.matmul(pyo.t[0:Q, g * 256:(g + 1) * 256], CT.t[:, g, cb:cb + Q], STp.t[:, g * 256:(g + 1) * 256],
                                   start=True, stop=True)
                return ins
            S.add("pe", fyo, reads=[CT, STp], writes=[pyo])
            def fyi(e, Q=Q, M=M, xd=xdt[k]):
                ins = None
                for hh in range(8):
                    ins = e.matmul(pyi.t[0:Q, hh * 64:(hh + 1) * 64], M.t[0:Q, hh * 64:hh * 64 + Q],
                                   xd.t[0:Q, hh * 64:(hh + 1) * 64], start=True, stop=True)
                return ins
            S.add("pe", fyi, reads=[M, xdt[k]], writes=[pyi])

        def ssd_Brest(blk, k):
            (seq, cb, Q, bi) = blk
            Sst = ST[l][seq]
            sm = sml[k]
            pyi, pyo, pup = PB[3], PB[4], PB[5]
            tt(Sst.t[:, :].rearrange("p (h d) -> p h d", d=64), Sst.t[:, :].rearrange("p (h d) -> p h d", d=64),
               sm.t[:, 16:24].unsqueeze(2).broadcast_to([128, 8, 64]), ALU.mult, [Sst, sm], [Sst])
            tt(Sst.t[:, :], Sst.t[:, :], pup.t[:, :], ALU.add, [Sst, pup], [Sst])
            act(STb2[bi % 2].t[:, :], Sst.t[:, :], AF.Copy, [Sst], [STb2[bi % 2]])
            tt(ytmp.t[0:Q, :].rearrange("p (h d) -> p h d", d=64), pyo.t[0:Q, :].rearrange("p (h d) -> p h d", d=64),
               sm.t[0:Q, 0:8].unsqueeze(2).broadcast_to([Q, 8, 64]), ALU.mult, [pyo, sm], [ytmp])
            tt(yraw.t[0:Q, :], ytmp.t[0:Q, :], pyi.t[0:Q, :], ALU.add, [ytmp, pyi], [yraw])

        def ssd_C(blk):
            (seq, cb, Q, bi) = blk
            pyt = PB[6]
            tr_multi([(pyt.t[:, c * 64:c * 64 + Q], yraw.t[0:Q, c * 128:(c + 1) * 128], ident.t[0:Q, 0:Q]) for c in range(4)],
                     [yraw, ident], [pyt])
            act(yT.t[:, :, cb:cb + Q], pyt.t[:, 0:256].rearrange("p (c q) -> p c q", q=64)[:, :, 0:Q], AF.Copy, [pyt], [yT])

        if cfg.get("no_ssd"):
            pass
        elif cfg.get("nopipe"):
            for i in range(nblk):
                ssd_A(blocks[i], 0)
                ssd_Bpe(blocks[i], 0)
                ssd_Brest(blocks[i], 0)
                ssd_C(blocks[i])
        else:
            for i in range(nblk + 2):
                if i < nblk:
                    ssd_A(blocks[i], i % 2)
                if 1 <= i <= nblk:
                    ssd_Bpe(blocks[i - 1], (i - 1) % 2)
                if i >= 2:
                    ssd_C(blocks[i - 2])
                if 1 <= i <= nblk:
                    ssd_Brest(blocks[i - 1], (i - 1) % 2)
        for c in range(4):
            stt(yT.t[:, c, 0:PW], xact.t[:, c, 0:PW], pcol(PV_DSK, l * 4 + c), yT.t[:, c, 0:PW], ALU.mult, ALU.add,
                [xact, pvec, yT], [yT])
            tt(yT.t[:, c, 0:PW], yT.t[:, c, 0:PW], zs.t[:, c, 0:PW], ALU.mult, [yT, zs], [yT])
        for g in range(2):
            pn = PB[g]
            for k in range(2):
                c = 2 * g + k
                q = msq[k]
                act(q.t[:, 0:PW], yT.t[:, c, 0:PW], AF.Square, [yT], [q])
                def fn(e, k=k, q=q, pn=pn):
                    return e.matmul(pn.t[:, 0:PW], onesb.t[:, :], q.t[:, 0:PW], start=(k == 0), stop=(k == 1))
                S.add("pe", fn, reads=[onesb, q], writes=[pn])
            lv, rs = (mlnv, mrstd) if g == 0 else (tA, tB)
            act(lv.t[:, 0:PW], pn.t[:, 0:PW], AF.Ln, [pn], [lv], bias=EPS, scale=1.0 / 256.0)
            act(rs.t[:, 0:PW], lv.t[:, 0:PW], AF.Exp, [lv], [rs], scale=-0.5)
            for k in range(2):
                c = 2 * g + k
                stt(hn.t[:, c, 0:PW], yT.t[:, c, 0:PW], pcol(PV_SNW, l * 4 + c), rs.t[:, 0:PW], ALU.mult, ALU.mult,
                    [yT, pvec, rs], [hn])

        gla_cur = [None]
        patb, pktb, pgub, pob = PB[0], PB[1], [PB[2], PB[3]], [PB[4], PB[5]]
        pktv = pbf(1)

        def gla_A(blk, k):
            (seq, cb, Q, bi) = blk
            def fat(e, Q=Q, cb=cb):
                ins = None
                for hd in range(4):
                    ins = e.matmul(patb.t[0:Q, hd * 64:hd * 64 + Q], kt.t[:, hd, cb:cb + Q], qt.t[:, hd, cb:cb + Q], start=True, stop=True)
                return ins
            S.add("pe", fat, reads=[kt, qt], writes=[patb])
            am = attm[k]
            tt(am.t[0:Q, :].rearrange("p (h i) -> p h i", i=64)[:, :, 0:Q],
               patb.t[0:Q, 0:256].rearrange("p (h i) -> p h i", i=64)[:, :, 0:Q],
               c64.t[0:Q, C_U:C_U + Q].unsqueeze(1).broadcast_to([Q, 4, Q]), ALU.mult, [patb, c64], [am])
            tr_multi([(pktv[0:Q, hd * 128:(hd + 1) * 128], kdT.t[:, hd, cb:cb + Q], identb.t[:, :]) for hd in range(4)],
                     [kdT, identb], [pktb])
            kk_ = kdtok[k]
            act(kk_.t[0:Q, :], pktv[0:Q, 0:512], AF.Copy, [pktb], [kk_])
            pgu = pgub[k]
            def fgu(e, Q=Q, bi=bi, kk_=kk_, pgu=pgu):
                ins = None
                for hd in range(4):
                    ins = e.matmul(pgu.t[:, hd * 128:(hd + 1) * 128], kk_.t[0:Q, hd * 128:(hd + 1) * 128],
                                   vtok.t[0:Q, bi, hd * 128:(hd + 1) * 128], start=True, stop=True)
                return ins
            S.add("pe", fgu, reads=[kk_, vtok], writes=[pgu])

        def gla_B(blk, k):
            (seq, cb, Q, bi) = blk
            Sg = SG[l][seq]
            if gla_cur[0] != seq:
                act(SGb2[(bi + 1) % 2].t[:, :], Sg.t[:, :], AF.Copy, [Sg], [SGb2[(bi + 1) % 2]])
                gla_cur[0] = seq
            SGp = SGb2[(bi + 1) % 2]
            am = attm[k]
            po = pob[k]
            pgu = pgub[k]
            def fo(e, Q=Q, cb=cb, bi=bi, am=am, po=po, SGp=SGp):
                ins = None
                for hd in range(4):
                    e.matmul(po.t[:, hd * 64:hd * 64 + Q], vtok.t[0:Q, bi, hd * 128:(hd + 1) * 128],
                             am.t[0:Q, hd * 64:hd * 64 + Q], start=True, stop=False)
                    ins = e.matmul(po.t[:, hd * 64:hd * 64 + Q], SGp.t[:, hd * 128:(hd + 1) * 128],
                                   qe.t[:, hd, cb:cb + Q], start=False, stop=True)
                return ins
            S.add("pe", fo, reads=[vtok, am, SGp, qe], writes=[po])
            tt(Sg.t[:, :].rearrange("p (h v) -> p h v", v=128), Sg.t[:, :].rearrange("p (h v) -> p h v", v=128),
               elast.t[:, :, bi:bi + 1].broadcast_to([128, 4, 128]), ALU.mult, [Sg, elast], [Sg])
            tt(Sg.t[:, :], Sg.t[:, :], pgu.t[:, :], ALU.add, [Sg, pgu], [Sg])
            act(SGb2[bi % 2].t[:, :], Sg.t[:, :], AF.Copy, [Sg], [SGb2[bi % 2]])
            act(oT.t[:, :, cb:cb + Q], po.t[:, 0:256].rearrange("p (h i) -> p h i", i=64)[:, :, 0:Q], AF.Copy, [po], [oT])

        for i in range(nblk + 1):
            if cfg.get("no_gla"):
                break
            if cfg.get("nopipe") or (len(units) > 1 and not cfg.get("pipe_small")):
                if i < nblk:
                    gla_A(blocks[i], 0)
                    gla_B(blocks[i], 0)
                continue
            if i < nblk:
                gla_A(blocks[i], i % 2)
            if i >= 1 and not cfg.get("no_glaB"):
                gla_B(blocks[i - 1], (i - 1) % 2)
        for hd in range(4):
            pn = PB[6 + hd % 2]
            q = msq[hd % 2]
            lv, rs = (mlnv, mrstd) if hd % 2 == 0 else (tA, tB)
            act(q.t[:, 0:PW], oT.t[:, hd, 0:PW], AF.Square, [oT], [q])
            mm(pn.t[:, 0:PW], [(onesb.t[:, :], q.t[:, 0:PW])], [onesb, q], [pn])
            act(lv.t[:, 0:PW], pn.t[:, 0:PW], AF.Ln, [pn], [lv], bias=EPS, scale=1.0 / 128.0)
            act(rs.t[:, 0:PW], lv.t[:, 0:PW], AF.Exp, [lv], [rs], scale=-0.5)
            stt(oT.t[:, hd, 0:PW], oT.t[:, hd, 0:PW], pcol(PV_HNW, l * 4 + hd), rs.t[:, 0:PW], ALU.mult, ALU.mult,
                [oT, pvec, rs], [oT])
            tt(hn.t[:, 4 + hd, 0:PW], oT.t[:, hd, 0:PW], gs.t[:, hd, 0:PW], ALU.mult, [oT, gs], [hn])
        for dc in range(NCH):
            wT, w = w_get()
            pb = PB[dc % 2]
            mm(pb.t[:, 0:PW], [(w[:, cc, :], hn.t[:, cc, 0:PW]) for cc in range(NCH)], [wT, hn], [pb])
            tt(h_t[:, dc, pc0:pc1], h_t[:, dc, pc0:pc1], pb.t[:, 0:PW], ALU.add, [h[dc], pb], [h[dc]])

    slot = [0]
    def nslot():
        slot[0] ^= 1
        return slot[0]

    for s in range(NSUP):
        W, nchunks = sup_layout(s)
        if s == 0:
            load_tokens(xs, 64, 0, nslot())
            load_tokens(meta, 16, 64, nslot())
            base = 80
        else:
            base = 0
        for g in range(8):
            load_tokens(xp[s * 1024 + g * 128:s * 1024 + (g + 1) * 128, :], 128, base + g * 128, nslot())
        for l in range(DEPTH):
            if DO_FFN:
                arena_switch()
                ffn(l, 0, W, nchunks)
            if DO_MIX:
                arena_switch()
                WS.cast_eng = cfg.get("mix_cast", "act")
                for (pc0, pc1, units) in mixer_passes(s):
                    mixer_pass(l, pc0, pc1, units)
                WS.cast_eng = "act"
            if DO_FFN:
                arena_switch()
                ffn(l, 2, W, nchunks)
        arena_switch()
        yfin = arena[:, 0:NCH * 512].rearrange("p (c w) -> p c w", w=512)
        for (c0, c1) in nchunks:
            n = c1 - c0
            for c in range(NCH):
                q = sqt[c % 2]
                act(q.t[:, 0:n], h_t[:, c, c0:c1], AF.Square, [h[c]], [q])
                def fn(e, c=c, q=q, n=n):
                    return e.matmul(PB[6].t[:, 0:n], onesb.t[:, :], q.t[:, 0:n], start=(c == 0), stop=(c == NCH - 1))
                S.add("pe", fn, reads=[onesb, q], writes=[PB[6]])
            act(lnv.t[:, 0:n], PB[6].t[:, 0:n], AF.Ln, [PB[6]], [lnv], bias=EPS, scale=1.0 / D)
            act(rstd.t[:, 0:n], lnv.t[:, 0:n], AF.Exp, [lnv], [rstd], scale=-0.5)
            for c in range(NCH):
                stt(yfin[:, c, 0:n], h_t[:, c, c0:c1], pcol(PV_LNF, c), rstd.t[:, 0:n], ALU.mult, ALU.mult,
                    [h[c], pvec, rstd], [xn])
            if s == 0 and c0 == 0:
                store_tokens(ys[0:64, :], 64, 0, nslot(), xn, yfin)
            else:
                pbase = (s * 1024 + (c0 - (80 if s == 0 else 0)))
                for g in range(n // 128):
                    store_tokens(yp[pbase + g * 128:pbase + (g + 1) * 128, :], 128, g * 128, nslot(), xn, yfin)
        arena_switch()

    if DO_MIX:
        for l in range(DEPTH):
            for q in range(2):
                dma(o_hg[q][l].rearrange("h k v -> k h v"), SG[l][q].t[:, :].rearrange("p (h v) -> p h v", v=128),
                    [SG[l][q]], [], key="osg%d%d" % (l, q))
                t0 = io[nslot()]
                tr_multi([(PB[0].t[:, c * 128:(c + 1) * 128], ST[l][q].t[:, c * 128:(c + 1) * 128], ident.t[:, :]) for c in range(4)],
                         [ST[l][q], ident], [PB[0]])
                cp(t0.t[:, 0:512], PB[0].t[:, :], [PB[0]], [t0])
                dma(o_ssm[q][l].rearrange("(c p) n -> p c n", p=128), t0.t[:, 0:512].rearrange("p (c n) -> p c n", n=128),
                    [t0], [], key="io%d" % slot[0])
                t1 = io[nslot()]
                tr_multi([(PB[1 + c // 4].t[0:3, (c % 4) * 128:(c % 4 + 1) * 128], TAILT[l][q][:, c, :], ident.t[:, :])
                          for c in range(8)], TAIL[l][q] + [ident], [PB[1], PB[2]])
                cp(t1.t[0:3, 0:512], PB[1].t[0:3, :], [PB[1]], [t1])
                cp(t1.t[0:3, 512:1024], PB[2].t[0:3, :], [PB[2]], [t1])
                dma(o_conv[q][l], t1.t[0:3, :], [t1], [], key="io%d" % slot[0])
    fence_reads = [io[0], io[1]] + [SG[l][q] for l in range(2) for q in range(2)]
    S.add("sp", None, reads=[], writes=fence_reads)

    keys = S.finalize()
    sems = {}
    for e in Sched.ENGS:
        sems[("eng", e)] = es.enter_context(nc.semaphore("se_" + e))
    for k in keys:
        sems[("dma", k)] = es.enter_context(nc.semaphore("sd_" + str(k)))
    semof = lambda sk: sems[sk]
    with nc.Block() as block:
        @block.sync
        def _(eng):
            S.run_stream("sp", eng, semof)

        @block.tensor
        def _(eng):
            S.run_stream("pe", eng, semof)

        @block.scalar
        def _(eng):
            S.run_stream("act", eng, semof)

        @block.vector
        def _(eng):
            S.run_stream("dve", eng, semof)

        @block.gpsimd
        def _(eng):
            S.run_stream("pool", eng, semof)
    es.close()
    return nc, S


_NC_CACHE = {}


def _consts():
    f32 = np.float32
    ident = np.eye(128, dtype=f32)
    c64 = np.zeros((64, NC64), f32)
    t = np.arange(64)[:, None]
    i = np.arange(64)[None, :]
    c64[:, C_U:C_U + 64] = (t <= i)
    c64[:, C_NEGU:C_NEGU + 64] = -(t <= i).astype(f32)
    c64[:, C_NBI:C_NBI + 64] = NEGBIG * np.eye(64, dtype=f32)
    c64[:, C_SLT:C_SLT + 64] = (t > i)
    c64[:, C_ONES:C_ONES + 128] = 1.0
    c64[:, C_SLM:C_SLM + 512] = np.tile((t > i).astype(f32), (1, 8))
    rmask = np.ones((128, 512), f32)
    rmask[:, ::64] = 0.0
    return ident, c64, rmask


def _colmaj(v):
    return np.ascontiguousarray(np.asarray(v, np.float32).reshape(-1, 128).T)


def _pack_params(p):
    pv = np.zeros((128, NPV), np.float32)
    for l in range(2):
        for kind, nm in enumerate(("ln_ffa_w", "ln_mix_w", "ln_ffb_w")):
            b = PV_LNW + (l * 3 + kind) * 8
            pv[:, b:b + 8] = _colmaj(p[nm][l])
        for k in range(4):
            b = PV_CW + (l * 4 + k) * 8
            pv[:, b:b + 8] = _colmaj(p["conv_w"][l, k])
        pv[:, PV_CB + l * 8:PV_CB + l * 8 + 8] = _colmaj(p["conv_b"][l])
        pv[:, PV_DSK + l * 4:PV_DSK + l * 4 + 4] = _colmaj(np.repeat(np.asarray(p["d_skip"][l], np.float32), 64))
        pv[:, PV_SNW + l * 4:PV_SNW + l * 4 + 4] = _colmaj(p["ssm_norm_w"][l])
        pv[:, PV_HNW + l * 4:PV_HNW + l * 4 + 4] = _colmaj(p["hg_norm_w"][l])
        pv[:, PV_HLB + l * 4:PV_HLB + l * 4 + 4] = _colmaj(p["hg_lb_raw"][l])
    pv[:, PV_LNF:PV_LNF + 8] = _colmaj(p["ln_f_w"])
    tk = np.zeros((128, 32), np.float32)
    for l in range(2):
        tk[:, l * 8:(l + 1) * 8] = np.asarray(p["dt_bias"][l], np.float32)[None, :]
        tk[:, 16 + l * 8:16 + (l + 1) * 8] = np.asarray(p["a_log"][l], np.float32)[None, :]
    return pv, tk


def run(inputs, cfg):
    key = tuple(sorted(cfg.items()))
    if key not in _NC_CACHE:
        _NC_CACHE[key] = build(cfg)
    nc, _ = _NC_CACHE[key]
    NSEG = cfg.get("nseg", 8)
    NPTOK = NSEG * 512
    f = lambda a: np.ascontiguousarray(np.asarray(a, dtype=np.float32))
    ident, c64, rmask = _consts()
    pv, tk = _pack_params(inputs)
    shared = {"meta": f(inputs["meta_tokens"]), "pvec": pv, "tokp": tk, "ident": ident, "c64": c64, "rmask": rmask}
    for nm in ("ffa_w_gate", "ffa_w_up", "ffa_w_down", "w_in", "w_out", "ffb_w_gate", "ffb_w_up", "ffb_w_down"):
        shared[nm] = f(inputs[nm])
    in_maps = []
    for b in range(8):
        m = dict(shared)
        m["xp"] = f(inputs["x_prompt"][b, :NPTOK])
        m["xs"] = f(inputs["x_sample"][b])
        m["sconv"] = f(inputs["state_conv"][:, b])
        m["sssm"] = f(np.asarray(inputs["state_ssm"])[:, b].reshape(2, 512, 128))
        m["shg"] = f(inputs["state_hgrn"][:, b])
        in_maps.append(m)
    res = run_bass_kernel_spmd(nc, in_maps, core_ids=list(range(8)))
    R = res.results
    y_prompt = np.stack([R[b]["yp"] for b in range(8)], 0)
    y_sample = np.stack([R[b]["ys"] for b in range(8)], 0)
    outs = [y_prompt, y_sample]
    for q in ("p", "s"):
        conv = np.stack([R[b]["conv_" + q] for b in range(8)], 1)
        ssm = np.stack([R[b]["ssm_" + q].reshape(2, 8, 64, 128) for b in range(8)], 1)
        hg = np.stack([R[b]["hg_" + q] for b in range(8)], 1)
        outs += [conv, ssm, hg]
    return tuple(np.ascontiguousarray(o.astype(np.float32)) for o in outs)


def kernel(**inputs):
    return run(inputs, {"nseg": 8, "depth": 2, "mixer": True, "ffn": True})
```

```python
import contextlib
import numpy as np
import concourse.bass as bass
import concourse.mybir as mybir
from concourse.bass_utils import run_bass_kernel_spmd

F32 = mybir.dt.float32
BF16 = mybir.dt.bfloat16
AF = mybir.ActivationFunctionType
ALU = mybir.AluOpType

D = 1024
NCH = 8
DFF = 2816
NF = 22
INC = 3592
EPS = 1e-6
NEGBIG = -30000.0
PV_LNW = 0
PV_LNF = 48
PV_CW = 56
PV_CB = 120
PV_DSK = 136
PV_SNW = 144
PV_HNW = 152
PV_HLB = 160
NPV = 168
C_U = 0
C_NEGU = 64
C_NBI = 128
C_SLT = 192
C_ONES = 256
C_SLM = 384
NC64 = 896


class Tl:
    def __init__(self, name, t):
        self.name = name
        self.t = t
        self.last_w = None
        self.readers = {}

    def __getitem__(self, k):
        return self.t[k]


class Op:
    __slots__ = ("eng", "fn", "deps", "key", "tok", "sig", "idx")


class Sched:
    ENGS = ("pe", "act", "dve", "pool", "sp")

    def __init__(self):
        self.streams = {e: [] for e in self.ENGS}
        self.n = 0

    def add(self, eng, fn, reads=(), writes=(), key=None):
        op = Op()
        op.eng, op.fn, op.key, op.sig, op.tok = eng, fn, key, False, None
        op.idx = self.n
        self.n += 1
        deps = set()
        for t in reads:
            if t.last_w is not None:
                deps.add(t.last_w)
        for t in writes:
            if t.last_w is not None:
                deps.add(t.last_w)
            for r in t.readers.values():
                deps.add(r)
        op.deps = deps
        rk = (eng, key)
        for t in reads:
            t.readers[rk] = op
        for t in writes:
            t.last_w = op
            t.readers = {}
        self.streams[eng].append(op)
        return op

    def finalize(self):
        for e in self.ENGS:
            for op in self.streams[e]:
                for d in op.deps:
                    d.sig = True
        dcnt = {}
        for e in self.ENGS:
            c = 0
            for op in self.streams[e]:
                if op.key is None:
                    if op.sig:
                        c += 1
                        op.tok = (("eng", e), c)
                else:
                    dcnt[op.key] = dcnt.get(op.key, 0) + 16
                    op.tok = (("dma", op.key), dcnt[op.key])
        return sorted(dcnt.keys(), key=str)

    def run_stream(self, e, engobj, semof):
        seen = {}
        for op in self.streams[e]:
            need = {}
            for d in op.deps:
                if d.key is None and d.eng == "pe" and e == "pe" and op.key is None:
                    continue
                sk, v = d.tok
                if seen.get(sk, 0) >= v:
                    continue
                if need.get(sk, 0) < v:
                    need[sk] = v
            for sk, v in need.items():
                engobj.wait_ge(semof(sk), v)
                seen[sk] = v
            if op.fn is not None:
                ins = op.fn(engobj)
                if op.key is not None:
                    ins.then_inc(semof(("dma", op.key)), 16)
                elif op.sig:
                    ins.then_inc(semof(("eng", e)), 1)


def interleave(*gens):
    gens = list(gens)
    while gens:
        for g in list(gens):
            try:
                next(g)
            except StopIteration:
                gens.remove(g)


def build(cfg):
    NSEG = cfg.get("nseg", 8)
    DEPTH = cfg.get("depth", 2)
    DO_MIX = cfg.get("mixer", True)
    DO_FFN = cfg.get("ffn", True)
    NPTOK = NSEG * 512
    nc = bass.Bass("TRN2", target_bir_lowering=False)
    S = Sched()
    es = contextlib.ExitStack()

    def din(name, shape, dt=F32):
        return nc.dram_tensor(name, list(shape), dt, kind="ExternalInput").ap()

    def dout(name, shape, dt=F32):
        return nc.dram_tensor(name, list(shape), dt, kind="ExternalOutput").ap()

    xp = din("xp", [NPTOK, D])
    xs = din("xs", [64, D])
    meta = din("meta", [16, D])
    sconv = din("sconv", [2, 3, D])
    sssm = din("sssm", [2, 512, 128])
    shg = din("shg", [2, 4, 128, 128])
    pvec_d = din("pvec", [128, NPV])
    tokp_d = din("tokp", [128, 32])
    ident_d = din("ident", [128, 128])
    c64_d = din("c64", [64, NC64])
    rmask_d = din("rmask", [128, 512])
    wts = {}
    for nm, shp in (("ffa_w_gate", [2, D, DFF]), ("ffa_w_up", [2, D, DFF]), ("ffa_w_down", [2, DFF, D]),
                    ("w_in", [2, D, INC]), ("w_out", [2, D, D]),
                    ("ffb_w_gate", [2, D, DFF]), ("ffb_w_up", [2, D, DFF]), ("ffb_w_down", [2, DFF, D])):
        wts[nm] = din(nm, shp)
    yp = dout("yp", [NPTOK, D])
    ys = dout("ys", [64, D])
    o_conv = [dout("conv_p", [2, 3, D]), dout("conv_s", [2, 3, D])]
    o_ssm = [dout("ssm_p", [2, 512, 128]), dout("ssm_s", [2, 512, 128])]
    o_hg = [dout("hg_p", [2, 4, 128, 128]), dout("hg_s", [2, 4, 128, 128])]

    def sb(name, shape, dt=F32):
        return es.enter_context(nc.sbuf_tensor("s_" + name, list(shape), dt))

    def T(name, shape, dt=F32):
        return Tl(name, sb(name, shape, dt))

    WMAX = 1104
    h = [Tl("h%d" % c, None) for c in range(NCH)]
    h_t = sb("h", [128, NCH, WMAX])
    for c in range(NCH):
        h[c].t = h_t[:, c, :]
    NST, NBF = 4, 4
    wst = [T("wst%d" % i, [128, 1024]) for i in range(NST)]
    wbf = [T("wbf%d" % i, [128, 1024], BF16) for i in range(NBF)]
    ident = T("ident", [128, 128])
    identb = T("identb", [128, 128], BF16)
    onesb = T("onesb", [128, 128], BF16)
    c64 = T("c64", [64, NC64])
    rmask = T("rmask", [128, 512])
    pvec = T("pvec", [128, NPV])
    tokp = T("tokp", [128, 32])
    abc = T("abc", [128, 16])
    lbt = T("lbt", [128, 8])
    omlb = T("omlb", [128, 8])
    wdt = T("wdt", [128, 2 * 8 * 8], BF16)
    wdt_f = T("wdt_f", [128, 2 * 8 * 8])
    io = [T("io%d" % i, [128, D]) for i in range(2)]
    ST = [[T("ST%d%d" % (l, q), [128, 512]) for q in range(2)] for l in range(2)]
    SG = [[T("SG%d%d" % (l, q), [128, 512]) for q in range(2)] for l in range(2)]
    TAILT = [[sb("TL%d%d" % (l, q), [128, 8, 3]) for q in range(2)] for l in range(2)]
    TAIL = [[[Tl("TL%d%d_%d" % (l, q, c), TAILT[l][q][:, c, :]) for c in range(8)] for q in range(2)] for l in range(2)]
    STb2 = [T("STb%d" % i, [128, 512], BF16) for i in range(2)]
    SGb2 = [T("SGb%d" % i, [128, 512], BF16) for i in range(2)]
    scr = T("scr", [128, 8])

    rem = nc.sbuf_bytes_remaining
    ARENA_F32 = ((rem - 2048) // 4) // 64 * 64
    if cfg.get("verbose"):
        print("sbuf remaining for arena", rem, "arena f32", ARENA_F32)
    arena = sb("arena", [128, ARENA_F32])
    apos = [0]

    def carve(shape, dt=F32, reset=False):
        if reset:
            apos[0] = 0
        n = int(np.prod(shape[1:]))
        nf = n if dt == F32 else (n + 1) // 2
        nf = (nf + 15) // 16 * 16
        a0 = apos[0]
        apos[0] += nf
        assert apos[0] <= ARENA_F32, ("arena overflow", apos[0], ARENA_F32)
        v = arena[:, a0:a0 + nf]
        if dt != F32:
            v = v.bitcast(dt)
        v = v[:, 0:n]
        if len(shape) == 3:
            v = v.rearrange("p (a b) -> p a b", b=shape[2])
        if shape[0] != 128:
            v = v[0:shape[0]]
        return v

    def AT(name, shape, dt=F32, reset=False):
        return Tl(name, carve(shape, dt, reset))

    xn = AT("xn", [128, NCH, WMAX], BF16, reset=True)
    actT = [None] * NF
    act_all = carve([128, NF, WMAX], BF16)
    for f in range(NF):
        actT[f] = Tl("act%d" % f, act_all[:, f, :])
    sqt = [AT("sqt%d" % i, [128, 512], BF16) for i in range(2)]
    lnv = AT("lnv", [128, 512])
    rstd = AT("rstd", [128, 512])
    sg = [AT("sg%d" % i, [128, 512]) for i in range(2)]
    ffn_end = apos[0]
    hn = AT("hn", [128, NCH, 512], BF16, reset=True)
    msq = [AT("msq%d" % i, [128, 512], BF16) for i in range(2)]
    xpre = [AT("xpre%d" % i, [128, 528]) for i in range(3)]
    cacc = [AT("cacc%d" % i, [128, 512]) for i in range(3)]
    xact = AT("xact", [128, 4, 512])
    BT = AT("BT", [128, 2, 512], BF16)
    CT = AT("CT", [128, 2, 512], BF16)
    zs = AT("zs", [128, 4, 512], BF16)
    yT = AT("yT", [128, 4, 512])
    cum = AT("cum", [128, 4, 512])
    qt = AT("qt", [128, 4, 512], BF16)
    kt = AT("kt", [128, 4, 512], BF16)
    qe = AT("qe", [128, 4, 512], BF16)
    kdT = AT("kdT", [128, 4, 512], BF16)
    gs = AT("gs", [128, 4, 512], BF16)
    vtok = AT("vtok", [64, 8, 512], BF16)
    oT = yT
    tA = AT("tA", [128, 512])
    tB = AT("tB", [128, 512])
    tC = AT("tC", [128, 512])
    elast = AT("elast", [128, 4, 8])
    dtt = AT("dtt", [64, 80])
    lat = AT("lat", [64, 80])
    dtw = AT("dtw", [64, 8])
    sml = [AT("sml%d" % i, [128, 24]) for i in range(2)]
    laU = AT("laU", [128, 512])
    lab = AT("lab", [128, 512])
    Lm = AT("Lm", [128, 512])
    Mm = [AT("Mm%d" % i, [64, 512], BF16) for i in range(2)]
    xdt = [AT("xdt%d" % i, [64, 512], BF16) for i in range(2)]
    wxdt = [AT("wxdt%d" % i, [64, 512], BF16) for i in range(2)]
    btok = [AT("btok%d" % i, [64, 256], BF16) for i in range(2)]
    yraw = AT("yraw", [64, 512])
    ytmp = AT("ytmp", [64, 512])
    kdtok = [AT("kdtok%d" % i, [64, 512], BF16) for i in range(2)]
    attm = [AT("attm%d" % i, [64, 256], BF16) for i in range(2)]
    if cfg.get("verbose"):
        print("arena use: ffn", ffn_end, "mixer", apos[0], "of", ARENA_F32)
    ARENA_TILES_F = [xn] + actT + sqt + [lnv, rstd] + sg
    ARENA_TILES_M = ([hn] + msq + xpre + cacc + [xact, BT, CT, zs, yT, cum, qt, kt, qe, kdT, gs, vtok,
                                                              tA, tB, tC, elast, dtt, lat, dtw, laU, lab, Lm, yraw, ytmp]
                     + sml + Mm + xdt + wxdt + btok + kdtok + attm)
    PB = [Tl("pb%d" % i, es.enter_context(nc.psum_tensor("pb%d" % i, [128, 512], F32))) for i in range(8)]

    def pbf(i):
        return PB[i].t[:, :].bitcast(BF16)

    def dma(out_ap, in_ap, reads, writes, key, eng="sp"):
        def fn(e, o=out_ap, i=in_ap):
            return e.dma_start(out=o, in_=i)
        S.add(eng, fn, reads=reads, writes=writes, key=key)

    def act(out_ap, in_ap, func, reads, writes, bias=None, scale=None):
        def fn(e, o=out_ap, i=in_ap, f=func, b=bias, s=scale):
            kw = {}
            if b is not None:
                kw["bias"] = b
            if s is not None:
                kw["scale"] = s
            return e.activation(o, i, f, **kw)
        S.add("act", fn, reads=reads, writes=writes)

    def tt(out_ap, a, b, op, reads, writes, eng="dve"):
        def fn(e, o=out_ap, a=a, b=b, op=op):
            return e.tensor_tensor(o, a, b, op)
        S.add(eng, fn, reads=reads, writes=writes)

    def ts(out_ap, a, s1, s2, op0, op1, reads, writes, eng="dve"):
        def fn(e, o=out_ap, a=a, s1=s1, s2=s2, op0=op0, op1=op1):
            if op1 is None:
                return e.tensor_scalar(o, a, s1, None, op0)
            return e.tensor_scalar(o, a, s1, s2, op0, op1)
        S.add(eng, fn, reads=reads, writes=writes)

    def stt(out_ap, a, sc, b, op0, op1, reads, writes):
        def fn(e, o=out_ap, a=a, sc=sc, b=b, op0=op0, op1=op1):
            return e.scalar_tensor_tensor(o, a, sc, b, op0, op1)
        S.add("dve", fn, reads=reads, writes=writes)

    def cp(out_ap, in_ap, reads, writes, eng="dve"):
        def fn(e, o=out_ap, i=in_ap):
            return e.tensor_copy(o, i)
        S.add(eng, fn, reads=reads, writes=writes)

    def memset(tile, ap, val, eng="dve"):
        def fn(e, a=ap, v=val):
            return e.memset(a, v)
        S.add(eng, fn, reads=[], writes=[tile])

    def mm(out_ap, pairs, reads, writes):
        def fn(e, o=out_ap, pairs=pairs):
            ins = None
            n = len(pairs)
            for i, (l, r) in enumerate(pairs):
                ins = e.matmul(o, l, r, start=(i == 0), stop=(i == n - 1))
            return ins
        S.add("pe", fn, reads=reads, writes=writes)

    def mm_multi(groups, reads, writes):
        def fn(e, groups=groups):
            ins = None
            for (o, pairs) in groups:
                n = len(pairs)
                for i, (l, r) in enumerate(pairs):
                    ins = e.matmul(o, l, r, start=(i == 0), stop=(i == n - 1))
            return ins
        S.add("pe", fn, reads=reads, writes=writes)

    def tr_multi(items, reads, writes):
        def fn(e, items=items):
            ins = None
            for (o, i, idn) in items:
                ins = e.transpose(o, i, idn)
            return ins
        S.add("pe", fn, reads=reads, writes=writes)

    def pcol(base, idx):
        return pvec.t[:, base + idx: base + idx + 1]

    DN_PARTS = [(0, 6), (6, 12), (12, 17), (17, 22)]

    class WS:
        items = []
        nload = 0
        ncast = 0
        nuse = 0
        cast_eng = "act"

    def w_panel(nm, l, kind, j):
        w = wts[nm]
        if kind == "gu":
            return (w[l].rearrange("(c p) f -> p c f", p=128)[:, :, j * 128:(j + 1) * 128], 8, 128)
        if kind == "dn":
            dc, part = j
            f0, f1 = DN_PARTS[part]
            return (w[l].rearrange("(c p) d -> p c d", p=128)[:, f0:f1, dc * 128:(dc + 1) * 128], f1 - f0, 128)
        if kind == "in":
            return (w[l].rearrange("(c p) f -> p c f", p=128)[:, :, j:j + 128], 8, 128)
        if kind == "out":
            return (w[l].rearrange("(c p) f -> p c f", p=128)[:, :, j * 128:(j + 1) * 128], 8, 128)
        raise ValueError

    def w_issue_load():
        k = WS.nload
        if k >= len(WS.items):
            return
        ap, R, C = WS.items[k]
        st = wst[k % NST]
        dma(st.t[:, 0:R * C].rearrange("p (r c) -> p r c", c=C), ap, [], [st], key="wst%d" % (k % NST))
        WS.nload += 1

    def w_issue_cast():
        k = WS.ncast
        if k >= len(WS.items):
            return
        ap, R, C = WS.items[k]
        st = wst[k % NST]
        bf = wbf[k % NBF]
        if WS.cast_eng == "act":
            act(bf.t[:, 0:R * C], st.t[:, 0:R * C], AF.Copy, [st], [bf])
        else:
            cp(bf.t[:, 0:R * C], st.t[:, 0:R * C], [st], [bf], eng=WS.cast_eng)
        WS.ncast += 1

    def w_get():
        k = WS.nuse
        CA = cfg.get("cast_ahead", 2)
        t_load = min(len(WS.items), k + CA + 4)
        t_cast = min(len(WS.items), k + CA + 1)
        while True:
            if WS.nload < t_load and WS.nload < WS.ncast + NST:
                w_issue_load()
            elif WS.ncast < t_cast and WS.ncast < WS.nload:
                w_issue_cast()
            else:
                break
        ap, R, C = WS.items[k]
        WS.nuse += 1
        bf = wbf[k % NBF]
        return bf, bf.t[:, 0:R * C].rearrange("p (r c) -> p r c", c=C)

    NSUP = NSEG // 2
    def sup_layout(s):
        if s == 0:
            return 1104, [(0, 80), (80, 592), (592, 1104)]
        return 1024, [(0, 512), (512, 1024)]

    IN_OFFS = ([c * 128 for c in range(4)] + [512 + c * 128 for c in range(8)]
               + [base + hd * 128 for pair in ((0, 1), (2, 3)) for base in (2056, 1544, 2568, 3080) for hd in pair])
    def mixer_passes(s):
        if s == 0:
            return [(0, 80, [(1, 0, 64, 64), (0, 64, 80, 16)]), (80, 592, [(0, 0, 512, 64)]), (592, 1104, [(0, 0, 512, 64)])]
        return [(0, 512, [(0, 0, 512, 64)]), (512, 1024, [(0, 0, 512, 64)])]

    for s in range(NSUP):
        for l in range(DEPTH):
            if DO_FFN:
                for j in range(22):
                    WS.items.append(w_panel("ffa_w_gate", l, "gu", j))
                    WS.items.append(w_panel("ffa_w_up", l, "gu", j))
                for dc in range(8):
                    for part in range(4):
                        WS.items.append(w_panel("ffa_w_down", l, "dn", (dc, part)))
            if DO_MIX:
                for _ in mixer_passes(s):
                    for off in IN_OFFS:
                        WS.items.append(w_panel("w_in", l, "in", off))
                    for j in range(8):
                        WS.items.append(w_panel("w_out", l, "out", j))
            if DO_FFN:
                for j in range(22):
                    WS.items.append(w_panel("ffb_w_gate", l, "gu", j))
                    WS.items.append(w_panel("ffb_w_up", l, "gu", j))
                for dc in range(8):
                    for part in range(4):
                        WS.items.append(w_panel("ffb_w_down", l, "dn", (dc, part)))

    dma(ident.t[:, :], ident_d, [], [ident], key="c0")
    dma(c64.t[:, :], c64_d, [], [c64], key="c1")
    dma(rmask.t[:, :], rmask_d, [], [rmask], key="c2")
    dma(pvec.t[:, :], pvec_d, [], [pvec], key="c3")
    dma(tokp.t[:, :], tokp_d, [], [tokp], key="c4")
    cp(identb.t[:, :], ident.t[:, :], [ident], [identb])
    memset(onesb, onesb.t[:, :], 1.0)
    memset(scr, scr.t[:, :], 0.0)
    for l in range(2):
        dma(wdt_f.t[:, l * 64:(l + 1) * 64].rearrange("p (c k) -> p c k", k=8),
            wts["w_in"][l].rearrange("(c p) f -> p c f", p=128)[:, :, 1536:1544], [], [wdt_f], key="c5")
    cp(wdt.t[:, :], wdt_f.t[:, :], [wdt_f], [wdt])
    act(abc.t[:, :], tokp.t[:, 16:32], AF.Exp, [tokp], [abc])
    ts(abc.t[:, :], abc.t[:, :], -1.0, None, ALU.mult, None, [abc], [abc])
    memset(lbt, lbt.t[:, :], 0.0)
    act(omlb.t[:, 0:8], pvec.t[:, PV_HLB:PV_HLB + 8], AF.Exp, [pvec], [omlb])
    tt(omlb.t[:, 0:4], omlb.t[:, 0:4], omlb.t[:, 4:8], ALU.add, [omlb], [omlb])
    def _recip(e):
        return e.reciprocal(omlb.t[:, 0:4], omlb.t[:, 0:4])
    S.add("dve", _recip, reads=[omlb], writes=[omlb])
    tt(lbt.t[:, 4:8], omlb.t[:, 4:8], omlb.t[:, 0:4], ALU.mult, [omlb, lbt], [lbt])
    ts(lbt.t[:, 4:8], lbt.t[:, 4:8], 1.0 - 1e-6, None, ALU.min, None, [lbt], [lbt])
    ts(omlb.t[:, :], lbt.t[:, :], -1.0, 1.0, ALU.mult, ALU.add, [lbt], [omlb])
    for l in range(2):
        memset(ST[l][0], ST[l][0].t[:, :], 0.0)
        memset(SG[l][0], SG[l][0].t[:, :], 0.0)
        def fms(e, l=l):
            return e.memset(TAILT[l][0][:, :, :], 0.0)
        S.add("dve", fms, reads=[], writes=TAIL[l][0])
    if DO_MIX:
        for l in range(DEPTH):
            dma(SG[l][1].t[:, :].rearrange("p (h v) -> p h v", v=128), shg[l].rearrange("h k v -> k h v"),
                [], [SG[l][1]], key="sg%d" % l)
            t0 = io[0]
            dma(t0.t[:, 0:512].rearrange("p (c n) -> p c n", n=128), sssm[l].rearrange("(c p) n -> p c n", p=128),
                [], [t0], key="io0")
            tr_multi([(PB[0].t[:, c * 128:(c + 1) * 128], t0.t[:, c * 128:(c + 1) * 128], ident.t[:, :]) for c in range(4)],
                     [t0, ident], [PB[0]])
            cp(ST[l][1].t[:, :], PB[0].t[:, :], [PB[0]], [ST[l][1]])
            t1 = io[1]
            dma(t1.t[0:3, :], sconv[l], [], [t1], key="io1")
            tr_multi([(PB[1].t[:, c * 4:c * 4 + 3], t1.t[0:3, c * 128:(c + 1) * 128], ident.t[0:3, 0:3]) for c in range(8)],
                     [t1, ident], [PB[1]])
            cp(TAILT[l][1][:, :, :], PB[1].t[:, 0:32].rearrange("p (c k) -> p c k", k=4)[:, :, 0:3], [PB[1]], TAIL[l][1])

    def rmsnorm(src_aps, src_tiles, dst_tile, dst_aps, wbase, n, sq2, lv, rs, pbank, out_f32_tile=None):
        for c in range(NCH):
            q = sq2[c % 2]
            act(q.t[:, 0:n], src_aps[c], AF.Square, [src_tiles[c]], [q])
            def fn(e, c=c, q=q):
                return e.matmul(pbank.t[:, 0:n], onesb.t[:, :], q.t[:, 0:n], start=(c == 0), stop=(c == NCH - 1))
            S.add("pe", fn, reads=[onesb, q], writes=[pbank])
        act(lv.t[:, 0:n], pbank.t[:, 0:n], AF.Ln, [pbank], [lv], bias=EPS, scale=1.0 / D)
        act(rs.t[:, 0:n], lv.t[:, 0:n], AF.Exp, [lv], [rs], scale=-0.5)
        for c in range(NCH):
            stt(dst_aps[c], src_aps[c], pcol(wbase, c), rs.t[:, 0:n], ALU.mult, ALU.mult,
                [src_tiles[c], pvec, rs], [dst_tile])

    def arena_switch(from_tiles=None, to_tiles=None):
        def fn(e):
            return e.memset(scr.t[:, 0:1], 0.0)
        S.add("dve", fn, reads=[], writes=ARENA_TILES_F + ARENA_TILES_M + [scr])

    def load_tokens(src_ap, ntok, col0, slot):
        t = io[slot]
        dma(t.t[0:ntok, :], src_ap, [], [t], key="io%d" % slot)
        for half in range(2):
            pb = PB[half]
            tr_multi([(pb.t[:, k * 128:k * 128 + ntok], t.t[0:ntok, (half * 4 + k) * 128:(half * 4 + k + 1) * 128],
                       ident.t[0:ntok, 0:ntok]) for k in range(4)], [t, ident], [pb])
            def fn(e, pb=pb, half=half):
                return e.activation(h_t[:, half * 4:half * 4 + 4, col0:col0 + ntok],
                                    pb.t[:, :].rearrange("p (k t) -> p k t", t=128)[:, :, 0:ntok], AF.Copy)
            S.add("act", fn, reads=[pb], writes=[h[half * 4 + k] for k in range(4)])

    def store_tokens(dst_ap, ntok, col0, slot, ytile, yview):
        t = io[slot]
        for half in range(2):
            pb = PB[2 + half]
            tr_multi([(pb.t[0:ntok, k * 128:(k + 1) * 128], yview[:, half * 4 + k, col0:col0 + ntok], ident.t[:, :])
                      for k in range(4)], [ytile, ident], [pb])
            def fn(e, pb=pb, half=half, t=t):
                return e.activation(t.t[0:ntok, half * 512:(half + 1) * 512], pb.t[0:ntok, :], AF.Copy)
            S.add("act", fn, reads=[pb], writes=[t])
        dma(dst_ap, t.t[0:ntok, :], [t], [], key="io%d" % slot)

    def ffn(l, kind, W, nchunks):
        wbase = PV_LNW + (l * 3 + kind) * 8
        for (c0, c1) in nchunks:
            n = c1 - c0
            rmsnorm([h_t[:, c, c0:c1] for c in range(NCH)], h, xn, [xn.t[:, c, c0:c1] for c in range(NCH)],
                    wbase, n, sqt, lnv, rstd, PB[6])
        it = 0
        for f in range(NF):
            wgT, wg = w_get()
            wuT, wu = w_get()
            for (c0, c1) in nchunks:
                n = c1 - c0
                pg = PB[it % 2]
                pu = PB[2 + it % 2]
                sgt = sg[it % 2]
                it += 1
                mm(pg.t[:, 0:n], [(wg[:, dc, :], xn.t[:, dc, c0:c1]) for dc in range(NCH)], [wgT, xn], [pg])
                mm(pu.t[:, 0:n], [(wu[:, dc, :], xn.t[:, dc, c0:c1]) for dc in range(NCH)], [wuT, xn], [pu])
                act(sgt.t[:, 0:n], pg.t[:, 0:n], AF.Silu, [pg], [sgt])
                tt(actT[f].t[:, c0:c1], sgt.t[:, 0:n], pu.t[:, 0:n], ALU.mult, [sgt, pu], [actT[f]])
        pybanks = [PB[4], PB[5], PB[7]]
        for dc in range(NCH):
            for pi, (f0, f1) in enumerate(DN_PARTS):
                wT, w = w_get()
                for ci, (c0, c1) in enumerate(nchunks):
                    n = c1 - c0
                    py = pybanks[ci]
                    def fn(e, w=w, f0=f0, f1=f1, c0=c0, c1=c1, n=n, py=py):
                        ins = None
                        for f in range(f0, f1):
                            ins = e.matmul(py.t[:, 0:n], w[:, f - f0, :], actT[f].t[:, c0:c1], start=(f == 0), stop=(f == NF - 1))
                        return ins
                    S.add("pe", fn, reads=[wT] + actT[f0:f1], writes=[py])
            for ci, (c0, c1) in enumerate(nchunks):
                n = c1 - c0
                py = pybanks[ci]
                stt(h_t[:, dc, c0:c1], py.t[:, 0:n], 0.5, h_t[:, dc, c0:c1], ALU.mult, ALU.add, [py, h[dc]], [h[dc]])

    def psum_split(bank, subs):
        def fn(e):
            return e.memset(scr.t[:, 1:2], 0.0)
        S.add("dve", fn, reads=[], writes=[bank] + subs + [scr])

    def mixer_pass(l, pc0, pc1, units):
        PW = pc1 - pc0
        lnb = PV_LNW + (l * 3 + 1) * 8
        rmsnorm([h_t[:, c, pc0:pc1] for c in range(NCH)], h, hn, [hn.t[:, c, 0:PW] for c in range(NCH)],
                lnb, PW, msq, tA, tB, PB[7])
        blocks = []
        ublocks = []
        for (seq, u0, u1, Q) in units:
            ub = []
            for b in range((u1 - u0) // Q):
                blk = (seq, u0 + b * Q, Q, len(blocks))
                blocks.append(blk)
                ub.append(blk)
            ublocks.append(ub)
        nblk = len(blocks)
        rot = [0]

        NBK = [0, 1, 2, 6, 7, 3]

        def nb():
            rot[0] += 1
            return PB[NBK[rot[0] % 6]]

        def inproj(w, wT, pb):
            mm(pb.t[:, 0:PW], [(w[:, dc, :], hn.t[:, dc, 0:PW]) for dc in range(NCH)], [wT, hn], [pb])

        for c in range(4):
            wT, w = w_get()
            pb = nb()
            inproj(w, wT, pb)
            act(zs.t[:, c, 0:PW], pb.t[:, 0:PW], AF.Silu, [pb], [zs])
        def conv_gen(c):
            wT, w = w_get()
            pb = nb()
            xb = xpre[c % 3]
            ca = cacc[c % 3]
            inproj(w, wT, pb)
            yield
            for ui, (seq, u0, u1, Q) in enumerate(units):
                n = u1 - u0
                bo = u0 + 3 * (ui + 1)
                tl = TAIL[l][seq][c]
                act(xb.t[:, bo - 3:bo], tl.t, AF.Copy, [tl], [xb])
                act(xb.t[:, bo:bo + n], pb.t[:, u0:u1], AF.Copy, [pb], [xb])
                yield
                act(ca.t[:, u0:u1], pb.t[:, u0:u1], AF.Identity, [pb, pvec], [ca],
                    bias=pcol(PV_CB, l * 8 + c), scale=pcol(PV_CW, (l * 4 + 3) * 8 + c))
                yield
                for k in range(3):
                    stt(ca.t[:, u0:u1], xb.t[:, bo - 3 + k:bo - 3 + k + n], pcol(PV_CW, (l * 4 + k) * 8 + c),
                        ca.t[:, u0:u1], ALU.mult, ALU.add, [xb, pvec, ca], [ca])
                    yield
                act(tl.t, xb.t[:, bo + n - 3:bo + n], AF.Copy, [xb], [tl])
            if c < 4:
                act(xact.t[:, c, 0:PW], ca.t[:, 0:PW], AF.Silu, [ca], [xact])
            elif c < 6:
                act(BT.t[:, c - 4, 0:PW], ca.t[:, 0:PW], AF.Silu, [ca], [BT])
            else:
                act(CT.t[:, c - 6, 0:PW], ca.t[:, 0:PW], AF.Silu, [ca], [CT])
            yield

        interleave(conv_gen(0), conv_gen(1), conv_gen(2))
        interleave(conv_gen(3), conv_gen(4), conv_gen(5))
        interleave(conv_gen(6), conv_gen(7))
        pdt = PB[3]
        groups = []
        for (seq, cb, Q, bi) in blocks:
            groups.append((pdt.t[0:Q, bi * 8:(bi + 1) * 8],
                           [(hn.t[:, dc, cb:cb + Q], wdt.t[:, (l * 8 + dc) * 8:(l * 8 + dc + 1) * 8]) for dc in range(NCH)]))
        mm_multi(groups, [hn, wdt], [pdt])
        bi0 = 0
        for (seq, u0, u1, Q) in units:
            nbk = (u1 - u0) // Q
            sl = slice(bi0 * 8, (bi0 + nbk) * 8)
            tt(dtt.t[0:Q, sl].rearrange("p (b k) -> p b k", k=8), pdt.t[0:Q, sl].rearrange("p (b k) -> p b k", k=8),
               tokp.t[0:Q, l * 8:(l + 1) * 8].unsqueeze(1).broadcast_to([Q, nbk, 8]), ALU.add, [pdt, tokp], [dtt])
            act(dtt.t[0:Q, sl], dtt.t[0:Q, sl], AF.Exp, [dtt], [dtt])
            act(dtt.t[0:Q, sl], dtt.t[0:Q, sl], AF.Ln, [dtt], [dtt], bias=1.0)
            tt(lat.t[0:Q, sl].rearrange("p (b k) -> p b k", k=8), dtt.t[0:Q, sl].rearrange("p (b k) -> p b k", k=8),
               abc.t[0:Q, l * 8:(l + 1) * 8].unsqueeze(1).broadcast_to([Q, nbk, 8]), ALU.mult, [dtt, abc], [lat])
            bi0 += nbk
        GT = [(tA, tB, tC), (laU, lab, Lm)]

        def head_gen(hd):
            gA, gB, gC = GT[hd % 2]
            wT, w = w_get()
            pb = nb()
            inproj(w, wT, pb)
            yield
            lbc = lbt.t[:, l * 4 + hd:l * 4 + hd + 1]
            omc = omlb.t[:, l * 4 + hd:l * 4 + hd + 1]
            act(gA.t[:, 0:PW], pb.t[:, 0:PW], AF.Exp, [pb], [gA], scale=-1.0)
            yield
            ts(gB.t[:, 0:PW], gA.t[:, 0:PW], 1.0, None, ALU.add, None, [gA], [gB])
            yield
            def frec(e_, PW=PW, gC=gC, gB=gB):
                return e_.reciprocal(gC.t[:, 0:PW], gB.t[:, 0:PW])
            S.add("dve", frec, reads=[gB], writes=[gC])
            yield
            stt(gC.t[:, 0:PW], gA.t[:, 0:PW], omc, gC.t[:, 0:PW], ALU.mult, ALU.mult, [gA, omlb, gC], [gC])
            yield
            act(gB.t[:, 0:PW], gB.t[:, 0:PW], AF.Ln, [gB], [gB])
            yield
            act(gA.t[:, 0:PW], gA.t[:, 0:PW], AF.Ln, [gA, lbt], [gA], bias=1.0, scale=lbc)
            yield
            tt(gA.t[:, 0:PW], gA.t[:, 0:PW], gB.t[:, 0:PW], ALU.subtract, [gA, gB], [gA])
            yield
            for (seq, u0, u1, Q) in units:
                def fscan(e_, u0=u0, u1=u1, hd=hd, gA=gA):
                    return e_.tensor_tensor_scan(cum.t[:, hd, u0:u1], rmask.t[:, 0:u1 - u0], gA.t[:, u0:u1], 0.0,
                                                 ALU.mult, ALU.add)
                S.add("dve", fscan, reads=[rmask, gA], writes=[cum])
            for (seq, u0, u1, Q) in units:
                nbk = (u1 - u0) // Q
                cv = cum.t[:, hd, u0:u1].rearrange("p (b q) -> p b q", q=Q)
                tv = gB.t[:, u0:u1].rearrange("p (b q) -> p b q", q=Q)
                tt(tv, cv, cv[:, :, Q // 2 - 1:Q // 2].broadcast_to([128, nbk, Q]), ALU.subtract, [cum], [gB])
            act(gB.t[:, 0:PW], gB.t[:, 0:PW], AF.Exp, [gB], [gB], scale=-1.0)
            yield
            tt(kt.t[:, hd, 0:PW], gC.t[:, 0:PW], gB.t[:, 0:PW], ALU.mult, [gC, gB], [kt])
            yield
            for (seq, u0, u1, Q) in units:
                nbk = (u1 - u0) // Q
                cv = cum.t[:, hd, u0:u1].rearrange("p (b q) -> p b q", q=Q)
                tv = gB.t[:, u0:u1].rearrange("p (b q) -> p b q", q=Q)
                tt(tv, cv, cv[:, :, Q - 1:Q].broadcast_to([128, nbk, Q]), ALU.subtract, [cum], [gB])
            act(gB.t[:, 0:PW], gB.t[:, 0:PW], AF.Exp, [gB], [gB], scale=-1.0)
            yield
            tt(kdT.t[:, hd, 0:PW], gC.t[:, 0:PW], gB.t[:, 0:PW], ALU.mult, [gC, gB], [kdT])
            yield
            bi0 = 0
            for (seq, u0, u1, Q) in units:
                nbk = (u1 - u0) // Q
                act(elast.t[:, hd, bi0:bi0 + nbk], cum.t[:, hd, u0:u1].rearrange("p (b q) -> p b q", q=Q)[:, :, Q - 1],
                    AF.Exp, [cum], [elast])
                bi0 += nbk
            for (seq, u0, u1, Q) in units:
                nbk = (u1 - u0) // Q
                cv = cum.t[:, hd, u0:u1].rearrange("p (b q) -> p b q", q=Q)
                tv = gB.t[:, u0:u1].rearrange("p (b q) -> p b q", q=Q)
                tt(tv, cv, cv[:, :, Q // 2 - 1:Q // 2].broadcast_to([128, nbk, Q]), ALU.subtract, [cum], [gB])
            act(gB.t[:, 0:PW], gB.t[:, 0:PW], AF.Exp, [gB], [gB])
            yield
            act(gC.t[:, 0:PW], cum.t[:, hd, 0:PW], AF.Exp, [cum], [gC])
            yield
            wT, w = w_get()
            pb = nb()
            inproj(w, wT, pb)
            yield
            act(gA.t[:, 0:PW], pb.t[:, 0:PW], AF.Silu, [pb], [gA])
            yield
            tt(qt.t[:, hd, 0:PW], gA.t[:, 0:PW], gB.t[:, 0:PW], ALU.mult, [gA, gB], [qt])
            yield
            tt(qe.t[:, hd, 0:PW], gA.t[:, 0:PW], gC.t[:, 0:PW], ALU.mult, [gA, gC], [qe])
            yield
            wT, w = w_get()
            gi = 0
            for ub in ublocks:
                for g0 in range(0, len(ub), 4):
                    grp = ub[g0:g0 + 4]
                    Q = grp[0][2]
                    pb = PB[4 + (gi + hd) % 2]
                    gi += 1
                    mm_multi([(pb.t[0:Q, j * 128:(j + 1) * 128], [(hn.t[:, dc, cb:cb + Q], w[:, dc, :]) for dc in range(NCH)])
                              for j, (seq, cb, Q_, bi) in enumerate(grp)], [wT, hn], [pb])
                    bi_0 = grp[0][3]
                    ng = len(grp)
                    act(vtok.t[0:Q, bi_0:bi_0 + ng, hd * 128:(hd + 1) * 128],
                        pb.t[0:Q, 0:ng * 128].rearrange("p (b v) -> p b v", v=128), AF.Copy, [pb], [vtok])
            yield
            wT, w = w_get()
            pb = nb()
            inproj(w, wT, pb)
            yield
            act(gs.t[:, hd, 0:PW], pb.t[:, 0:PW], AF.Silu, [pb], [gs])
            yield


        interleave(head_gen(0), head_gen(1))
        interleave(head_gen(2), head_gen(3))

        P1cbm = P1sm = P1bt = PB[1]
        p1 = PB[1].t
        p1bf = pbf(1)
        segb = [PB[0], PB[0]]
        p7bf = pbf(7)
        ssd_cur = [None]

        def ssd_A(blk, k):
            (seq, cb, Q, bi) = blk
            la_b = lat.t[0:Q, bi * 8:(bi + 1) * 8]
            tt(laU.t[0:Q, :].rearrange("p (h i) -> p h i", i=64), la_b.unsqueeze(2).broadcast_to([Q, 8, 64]),
               c64.t[0:Q, C_U:C_U + 64].unsqueeze(1).broadcast_to([Q, 8, 64]), ALU.mult, [lat, c64], [laU], eng="pool")
            cp(lab.t[0:Q, :].rearrange("p (h i) -> p h i", i=64), la_b.unsqueeze(2).broadcast_to([Q, 8, 64]),
               [lat], [lab], eng="pool")
            pseg = segb[k]
            segv = pseg.t[0:Q, :].rearrange("p (h i) -> p h i", i=64)[:, :, 0:Q]
            def fseg(e, Q=Q, pseg=pseg):
                e.matmul(pseg.t[0:Q, :], c64.t[0:Q, C_ONES:C_ONES + Q], laU.t[0:Q, :], start=True, stop=False)
                e.matmul(pseg.t[0:Q, :], c64.t[0:Q, C_NEGU:C_NEGU + Q], lab.t[0:Q, :], start=False, stop=False)
                return e.matmul(pseg.t[0:Q, :], c64.t[0:Q, C_NBI:C_NBI + Q], c64.t[0:Q, C_SLM:C_SLM + 512],
                                start=False, stop=True)
            S.add("pe", fseg, reads=[c64, laU, lab], writes=[pseg])
            def fsm(e, Q=Q, la_b=la_b):
                e.matmul(p1[0:Q, 256:264], c64.t[0:Q, C_U:C_U + Q], la_b, start=True, stop=True)
                e.matmul(p1[0:Q, 264:272], c64.t[0:Q, C_SLT:C_SLT + Q], la_b, start=True, stop=True)
                return e.matmul(p1[:, 272:280], c64.t[0:Q, C_ONES:C_ONES + 128], la_b, start=True, stop=True)
            S.add("pe", fsm, reads=[c64, lat], writes=[P1sm])
            def fcb(e, Q=Q, cb=cb):
                ins = None
                for g in range(2):
                    ins = e.matmul(p1[0:Q, g * 64:g * 64 + Q], BT.t[:, g, cb:cb + Q], CT.t[:, g, cb:cb + Q], start=True, stop=True)
                return ins
            S.add("pe", fcb, reads=[BT, CT], writes=[P1cbm])
            tr_multi([(p7bf[0:Q, g * 128:(g + 1) * 128], BT.t[:, g, cb:cb + Q], identb.t[:, :]) for g in range(2)],
                     [BT, identb], [PB[7]])
            sm = sml[k]
            act(sm.t[0:Q, 0:16], p1[0:Q, 256:272], AF.Exp, [P1sm], [sm])
            act(sm.t[:, 16:24], p1[:, 272:280], AF.Exp, [P1sm], [sm])
            act(Lm.t[0:Q, :].rearrange("p (h i) -> p h i", i=64)[:, :, 0:Q], segv, AF.Exp, [pseg], [Lm])
            M = Mm[k]
            tt(M.t[0:Q, :].rearrange("p (g r i) -> p g r i", r=4, i=64)[:, :, :, 0:Q],
               Lm.t[0:Q, :].rearrange("p (g r i) -> p g r i", r=4, i=64)[:, :, :, 0:Q],
               p1[0:Q, 0:128].rearrange("p (g i) -> p g i", i=64)[:, :, 0:Q].unsqueeze(2).broadcast_to([Q, 2, 4, Q]),
               ALU.mult, [Lm, P1cbm], [M])
            pxt = PB[2]
            tr_multi([(pxt.t[0:Q, c * 128:(c + 1) * 128], xact.t[:, c, cb:cb + Q], ident.t[:, :]) for c in range(4)],
                     [xact, ident], [pxt])
            bk = btok[k]
            act(bk.t[0:Q, :], p7bf[0:Q, 0:256], AF.Copy, [PB[7]], [bk])
            dt_b = dtt.t[0:Q, bi * 8:(bi + 1) * 8]
            tt(dtw.t[0:Q, :], dt_b, sm.t[0:Q, 8:16], ALU.mult, [dtt, sm], [dtw])
            tt(xdt[k].t[0:Q, :].rearrange("p (h d) -> p h d", d=64), pxt.t[0:Q, :].rearrange("p (h d) -> p h d", d=64),
               dt_b.unsqueeze(2).broadcast_to([Q, 8, 64]), ALU.mult, [pxt, dtt], [xdt[k]])
            tt(wxdt[k].t[0:Q, :].rearrange("p (h d) -> p h d", d=64), pxt.t[0:Q, :].rearrange("p (h d) -> p h d", d=64),
               dtw.t[0:Q, :].unsqueeze(2).broadcast_to([Q, 8, 64]), ALU.mult, [pxt, dtw], [wxdt[k]])

        def ssd_Bpe(blk, k):
            (seq, cb, Q, bi) = blk
            Sst = ST[l][seq]
            if ssd_cur[0] != seq:
                act(STb2[(bi + 1) % 2].t[:, :], Sst.t[:, :], AF.Copy, [Sst], [STb2[(bi + 1) % 2]])
                ssd_cur[0] = seq
            STp = STb2[(bi + 1) % 2]
            M = Mm[k]
            pyi, pyo, pup = PB[3], PB[4], PB[5]
            def fup(e, Q=Q, bk=btok[k], wx=wxdt[k]):
                ins = None
                for g in range(2):
                    ins = e.matmul(pup.t[:, g * 256:(g + 1) * 256], bk.t[0:Q, g * 128:(g + 1) * 128],
                                   wx.t[0:Q, g * 256:(g + 1) * 256], start=True, stop=True)
                return ins
            S.add("pe", fup, reads=[btok[k], wxdt[k]], writes=[pup])
            def fyo(e, Q=Q, cb=cb, STp=STp):
                ins = None
                for g in range(2):
                    ins = e.matmul(pyo.t[0:Q, g * 256:(g + 1) * 256], CT.t[:, g, cb:cb + Q], STp.t[:, g * 256:(g + 1) * 256],
                                   start=True, stop=True)
                return ins
            S.add("pe", fyo, reads=[CT, STp], writes=[pyo])
            def fyi(e, Q=Q, M=M, xd=xdt[k]):
                ins = None
                for hh in range(8):
                    ins = e.matmul(pyi.t[0:Q, hh * 64:(hh + 1) * 64], M.t[0:Q, hh * 64:hh * 64 + Q],
                                   xd.t[0:Q, hh * 64:(hh + 1) * 64], start=True, stop=True)
                return ins
            S.add("pe", fyi, reads=[M, xdt[k]], writes=[pyi])

        def ssd_Brest(blk, k):
            (seq, cb, Q, bi) = blk
            Sst = ST[l][seq]
            sm = sml[k]
            pyi, pyo, pup = PB[3], PB[4], PB[5]
            tt(Sst.t[:, :].rearrange("p (h d) -> p h d", d=64), Sst.t[:, :].rearrange("p (h d) -> p h d", d=64),
               sm.t[:, 16:24].unsqueeze(2).broadcast_to([128, 8, 64]), ALU.mult, [Sst, sm], [Sst])
            tt(Sst.t[:, :], Sst.t[:, :], pup.t[:, :], ALU.add, [Sst, pup], [Sst])
            act(STb2[bi % 2].t[:, :], Sst.t[:, :], AF.Copy, [Sst], [STb2[bi % 2]])
            tt(ytmp.t[0:Q, :].rearrange("p (h d) -> p h d", d=64), pyo.t[0:Q, :].rearrange("p (h d) -> p h d", d=64),
               sm.t[0:Q, 0:8].unsqueeze(2).broadcast_to([Q, 8, 64]), ALU.mult, [pyo, sm], [ytmp])
            tt(yraw.t[0:Q, :], ytmp.t[0:Q, :], pyi.t[0:Q, :], ALU.add, [ytmp, pyi], [yraw])

        def ssd_C(blk):
            (seq, cb, Q, bi) = blk
            pyt = PB[6]
            tr_multi([(pyt.t[:, c * 64:c * 64 + Q], yraw.t[0:Q, c * 128:(c + 1) * 128], ident.t[0:Q, 0:Q]) for c in range(4)],
                     [yraw, ident], [pyt])
            act(yT.t[:, :, cb:cb + Q], pyt.t[:, 0:256].rearrange("p (c q) -> p c q", q=64)[:, :, 0:Q], AF.Copy, [pyt], [yT])

        if cfg.get("no_ssd"):
            pass
        elif cfg.get("nopipe"):
            for i in range(nblk):
                ssd_A(blocks[i], 0)
                ssd_Bpe(blocks[i], 0)
                ssd_Brest(blocks[i], 0)
                ssd_C(blocks[i])
        else:
            for i in range(nblk + 2):
                if i < nblk:
                    ssd_A(blocks[i], i % 2)
                if 1 <= i <= nblk:
                    ssd_Bpe(blocks[i - 1], (i - 1) % 2)
                if i >= 2:
                    ssd_C(blocks[i - 2])
                if 1 <= i <= nblk:
                    ssd_Brest(blocks[i - 1], (i - 1) % 2)
        for c in range(4):
            stt(yT.t[:, c, 0:PW], xact.t[:, c, 0:PW], pcol(PV_DSK, l * 4 + c), yT.t[:, c, 0:PW], ALU.mult, ALU.add,
                [xact, pvec, yT], [yT])
            tt(yT.t[:, c, 0:PW], yT.t[:, c, 0:PW], zs.t[:, c, 0:PW], ALU.mult, [yT, zs], [yT])
        for g in range(2):
            pn = PB[g]
            for k in range(2):
                c = 2 * g + k
                q = msq[k]
                act(q.t[:, 0:PW], yT.t[:, c, 0:PW], AF.Square, [yT], [q])
                def fn(e, k=k, q=q, pn=pn):
                    return e.matmul(pn.t[:, 0:PW], onesb.t[:, :], q.t[:, 0:PW], start=(k == 0), stop=(k == 1))
                S.add("pe", fn, reads=[onesb, q], writes=[pn])
            lv, rs = (tC, Lm) if g == 0 else (tA, tB)
            act(lv.t[:, 0:PW], pn.t[:, 0:PW], AF.Ln, [pn], [lv], bias=EPS, scale=1.0 / 256.0)
            act(rs.t[:, 0:PW], lv.t[:, 0:PW], AF.Exp, [lv], [rs], scale=-0.5)
            for k in range(2):
                c = 2 * g + k
                stt(hn.t[:, c, 0:PW], yT.t[:, c, 0:PW], pcol(PV_SNW, l * 4 + c), rs.t[:, 0:PW], ALU.mult, ALU.mult,
                    [yT, pvec, rs], [hn])

        gla_cur = [None]
        patb, pktb, pgub, pob = PB[0], PB[1], [PB[2], PB[3]], [PB[4], PB[5]]
        pktv = pbf(1)

        def gla_A(blk, k):
            (seq, cb, Q, bi) = blk
            def fat(e, Q=Q, cb=cb):
                ins = None
                for hd in range(4):
                    ins = e.matmul(patb.t[0:Q, hd * 64:hd * 64 + Q], kt.t[:, hd, cb:cb + Q], qt.t[:, hd, cb:cb + Q], start=True, stop=True)
                return ins
            S.add("pe", fat, reads=[kt, qt], writes=[patb])
            am = attm[k]
            tt(am.t[0:Q, :].rearrange("p (h i) -> p h i", i=64)[:, :, 0:Q],
               patb.t[0:Q, 0:256].rearrange("p (h i) -> p h i", i=64)[:, :, 0:Q],
               c64.t[0:Q, C_U:C_U + Q].unsqueeze(1).broadcast_to([Q, 4, Q]), ALU.mult, [patb, c64], [am])
            tr_multi([(pktv[0:Q, hd * 128:(hd + 1) * 128], kdT.t[:, hd, cb:cb + Q], identb.t[:, :]) for hd in range(4)],
                     [kdT, identb], [pktb])
            kk_ = kdtok[k]
            act(kk_.t[0:Q, :], pktv[0:Q, 0:512], AF.Copy, [pktb], [kk_])
            pgu = pgub[k]
            def fgu(e, Q=Q, bi=bi, kk_=kk_, pgu=pgu):
                ins = None
                for hd in range(4):
                    ins = e.matmul(pgu.t[:, hd * 128:(hd + 1) * 128], kk_.t[0:Q, hd * 128:(hd + 1) * 128],
                                   vtok.t[0:Q, bi, hd * 128:(hd + 1) * 128], start=True, stop=True)
                return ins
            S.add("pe", fgu, reads=[kk_, vtok], writes=[pgu])

        def gla_B(blk, k):
            (seq, cb, Q, bi) = blk
            Sg = SG[l][seq]
            if gla_cur[0] != seq:
                act(SGb2[(bi + 1) % 2].t[:, :], Sg.t[:, :], AF.Copy, [Sg], [SGb2[(bi + 1) % 2]])
                gla_cur[0] = seq
            SGp = SGb2[(bi + 1) % 2]
            am = attm[k]
            po = pob[k]
            pgu = pgub[k]
            def fo(e, Q=Q, cb=cb, bi=bi, am=am, po=po, SGp=SGp):
                ins = None
                for hd in range(4):
                    e.matmul(po.t[:, hd * 64:hd * 64 + Q], vtok.t[0:Q, bi, hd * 128:(hd + 1) * 128],
                             am.t[0:Q, hd * 64:hd * 64 + Q], start=True, stop=False)
                    ins = e.matmul(po.t[:, hd * 64:hd * 64 + Q], SGp.t[:, hd * 128:(hd + 1) * 128],
                                   qe.t[:, hd, cb:cb + Q], start=False, stop=True)
                return ins
            S.add("pe", fo, reads=[vtok, am, SGp, qe], writes=[po])
            tt(Sg.t[:, :].rearrange("p (h v) -> p h v", v=128), Sg.t[:, :].rearrange("p (h v) -> p h v", v=128),
               elast.t[:, :, bi:bi + 1].broadcast_to([128, 4, 128]), ALU.mult, [Sg, elast], [Sg])
            tt(Sg.t[:, :], Sg.t[:, :], pgu.t[:, :], ALU.add, [Sg, pgu], [Sg])
            act(SGb2[bi % 2].t[:, :], Sg.t[:, :], AF.Copy, [Sg], [SGb2[bi % 2]])
            act(oT.t[:, :, cb:cb + Q], po.t[:, 0:256].rearrange("p (h i) -> p h i", i=64)[:, :, 0:Q], AF.Copy, [po], [oT])

        for i in range(nblk + 1):
            if cfg.get("no_gla"):
                break
            if cfg.get("nopipe") or (len(units) > 1 and not cfg.get("pipe_small")):
                if i < nblk:
                    gla_A(blocks[i], 0)
                    gla_B(blocks[i], 0)
                continue
            if i < nblk:
                gla_A(blocks[i], i % 2)
            if i >= 1 and not cfg.get("no_glaB"):
                gla_B(blocks[i - 1], (i - 1) % 2)
        for hd in range(4):
            pn = PB[6 + hd % 2]
            q = msq[hd % 2]
            lv, rs = (tC, Lm) if hd % 2 == 0 else (tA, tB)
            act(q.t[:, 0:PW], oT.t[:, hd, 0:PW], AF.Square, [oT], [q])
            mm(pn.t[:, 0:PW], [(onesb.t[:, :], q.t[:, 0:PW])], [onesb, q], [pn])
            act(lv.t[:, 0:PW], pn.t[:, 0:PW], AF.Ln, [pn], [lv], bias=EPS, scale=1.0 / 128.0)
            act(rs.t[:, 0:PW], lv.t[:, 0:PW], AF.Exp, [lv], [rs], scale=-0.5)
            stt(oT.t[:, hd, 0:PW], oT.t[:, hd, 0:PW], pcol(PV_HNW, l * 4 + hd), rs.t[:, 0:PW], ALU.mult, ALU.mult,
                [oT, pvec, rs], [oT])
            tt(hn.t[:, 4 + hd, 0:PW], oT.t[:, hd, 0:PW], gs.t[:, hd, 0:PW], ALU.mult, [oT, gs], [hn])
        for dc in range(NCH):
            wT, w = w_get()
            pb = PB[dc % 2]
            mm(pb.t[:, 0:PW], [(w[:, cc, :], hn.t[:, cc, 0:PW]) for cc in range(NCH)], [wT, hn], [pb])
            tt(h_t[:, dc, pc0:pc1], h_t[:, dc, pc0:pc1], pb.t[:, 0:PW], ALU.add, [h[dc], pb], [h[dc]])

    slot = [0]
    def nslot():
        slot[0] ^= 1
        return slot[0]

    for s in range(NSUP):
        W, nchunks = sup_layout(s)
        if s == 0:
            load_tokens(xs, 64, 0, nslot())
            load_tokens(meta, 16, 64, nslot())
            base = 80
        else:
            base = 0
        for g in range(8):
            load_tokens(xp[s * 1024 + g * 128:s * 1024 + (g + 1) * 128, :], 128, base + g * 128, nslot())
        for l in range(DEPTH):
            if DO_FFN:
                arena_switch()
                ffn(l, 0, W, nchunks)
            if DO_MIX:
                arena_switch()
                WS.cast_eng = cfg.get("mix_cast", "act")
                for (pc0, pc1, units) in mixer_passes(s):
                    mixer_pass(l, pc0, pc1, units)
                WS.cast_eng = "act"
            if DO_FFN:
                arena_switch()
                ffn(l, 2, W, nchunks)
        arena_switch()
        yfin = arena[:, 0:NCH * 512].rearrange("p (c w) -> p c w", w=512)
        for (c0, c1) in nchunks:
            n = c1 - c0
            for c in range(NCH):
                q = sqt[c % 2]
                act(q.t[:, 0:n], h_t[:, c, c0:c1], AF.Square, [h[c]], [q])
                def fn(e, c=c, q=q, n=n):
                    return e.matmul(PB[6].t[:, 0:n], onesb.t[:, :], q.t[:, 0:n], start=(c == 0), stop=(c == NCH - 1))
                S.add("pe", fn, reads=[onesb, q], writes=[PB[6]])
            act(lnv.t[:, 0:n], PB[6].t[:, 0:n], AF.Ln, [PB[6]], [lnv], bias=EPS, scale=1.0 / D)
            act(rstd.t[:, 0:n], lnv.t[:, 0:n], AF.Exp, [lnv], [rstd], scale=-0.5)
            for c in range(NCH):
                stt(yfin[:, c, 0:n], h_t[:, c, c0:c1], pcol(PV_LNF, c), rstd.t[:, 0:n], ALU.mult, ALU.mult,
                    [h[c], pvec, rstd], [xn])
            if s == 0 and c0 == 0:
                store_tokens(ys[0:64, :], 64, 0, nslot(), xn, yfin)
            else:
                pbase = (s * 1024 + (c0 - (80 if s == 0 else 0)))
                for g in range(n // 128):
                    store_tokens(yp[pbase + g * 128:pbase + (g + 1) * 128, :], 128, g * 128, nslot(), xn, yfin)
        arena_switch()

    if DO_MIX:
        for l in range(DEPTH):
            for q in range(2):
                dma(o_hg[q][l].rearrange("h k v -> k h v"), SG[l][q].t[:, :].rearrange("p (h v) -> p h v", v=128),
                    [SG[l][q]], [], key="osg%d%d" % (l, q))
                t0 = io[nslot()]
                tr_multi([(PB[0].t[:, c * 128:(c + 1) * 128], ST[l][q].t[:, c * 128:(c + 1) * 128], ident.t[:, :]) for c in range(4)],
                         [ST[l][q], ident], [PB[0]])
                cp(t0.t[:, 0:512], PB[0].t[:, :], [PB[0]], [t0])
                dma(o_ssm[q][l].rearrange("(c p) n -> p c n", p=128), t0.t[:, 0:512].rearrange("p (c n) -> p c n", n=128),
                    [t0], [], key="io%d" % slot[0])
                t1 = io[nslot()]
                tr_multi([(PB[1 + c // 4].t[0:3, (c % 4) * 128:(c % 4 + 1) * 128], TAILT[l][q][:, c, :], ident.t[:, :])
                          for c in range(8)], TAIL[l][q] + [ident], [PB[1], PB[2]])
                cp(t1.t[0:3, 0:512], PB[1].t[0:3, :], [PB[1]], [t1])
                cp(t1.t[0:3, 512:1024], PB[2].t[0:3, :], [PB[2]], [t1])
                dma(o_conv[q][l], t1.t[0:3, :], [t1], [], key="io%d" % slot[0])
    fence_reads = [io[0], io[1]] + [SG[l][q] for l in range(2) for q in range(2)]
    S.add("sp", None, reads=[], writes=fence_reads)

    keys = S.finalize()
    sems = {}
    for e in Sched.ENGS:
        sems[("eng", e)] = es.enter_context(nc.semaphore("se_" + e))
    for k in keys:
        sems[("dma", k)] = es.enter_context(nc.semaphore("sd_" + str(k)))
    semof = lambda sk: sems[sk]
    with nc.Block() as block:
        @block.sync
        def _(eng):
            S.run_stream("sp", eng, semof)

        @block.tensor
        def _(eng):
            S.run_stream("pe", eng, semof)

        @block.scalar
        def _(eng):
            S.run_stream("act", eng, semof)

        @block.vector
        def _(eng):
            S.run_stream("dve", eng, semof)

        @block.gpsimd
        def _(eng):
            S.run_stream("pool", eng, semof)
    es.close()
    return nc, S


_NC_CACHE = {}


def _consts():
    f32 = np.float32
    ident = np.eye(128, dtype=f32)
    c64 = np.zeros((64, NC64), f32)
    t = np.arange(64)[:, None]
    i = np.arange(64)[None, :]
    c64[:, C_U:C_U + 64] = (t <= i)
    c64[:, C_NEGU:C_NEGU + 64] = -(t <= i).astype(f32)
    c64[:, C_NBI:C_NBI + 64] = NEGBIG * np.eye(64, dtype=f32)
    c64[:, C_SLT:C_SLT + 64] = (t > i)
    c64[:, C_ONES:C_ONES + 128] = 1.0
    c64[:, C_SLM:C_SLM + 512] = np.tile((t > i).astype(f32), (1, 8))
    rmask = np.ones((128, 512), f32)
    rmask[:, ::64] = 0.0
    return ident, c64, rmask


def _colmaj(v):
    return np.ascontiguousarray(np.asarray(v, np.float32).reshape(-1, 128).T)


def _pack_params(p):
    pv = np.zeros((128, NPV), np.float32)
    for l in range(2):
        for kind, nm in enumerate(("ln_ffa_w", "ln_mix_w", "ln_ffb_w")):
            b = PV_LNW + (l * 3 + kind) * 8
            pv[:, b:b + 8] = _colmaj(p[nm][l])
        for k in range(4):
            b = PV_CW + (l * 4 + k) * 8
            pv[:, b:b + 8] = _colmaj(p["conv_w"][l, k])
        pv[:, PV_CB + l * 8:PV_CB + l * 8 + 8] = _colmaj(p["conv_b"][l])
        pv[:, PV_DSK + l * 4:PV_DSK + l * 4 + 4] = _colmaj(np.repeat(np.asarray(p["d_skip"][l], np.float32), 64))
        pv[:, PV_SNW + l * 4:PV_SNW + l * 4 + 4] = _colmaj(p["ssm_norm_w"][l])
        pv[:, PV_HNW + l * 4:PV_HNW + l * 4 + 4] = _colmaj(p["hg_norm_w"][l])
        pv[:, PV_HLB + l * 4:PV_HLB + l * 4 + 4] = _colmaj(p["hg_lb_raw"][l])
    pv[:, PV_LNF:PV_LNF + 8] = _colmaj(p["ln_f_w"])
    tk = np.zeros((128, 32), np.float32)
    for l in range(2):
        tk[:, l * 8:(l + 1) * 8] = np.asarray(p["dt_bias"][l], np.float32)[None, :]
        tk[:, 16 + l * 8:16 + (l + 1) * 8] = np.asarray(p["a_log"][l], np.float32)[None, :]
    return pv, tk


def run(inputs, cfg):
    key = tuple(sorted(cfg.items()))
    if key not in _NC_CACHE:
        _NC_CACHE[key] = build(cfg)
    nc, _ = _NC_CACHE[key]
    NSEG = cfg.get("nseg", 8)
    NPTOK = NSEG * 512
    f = lambda a: np.ascontiguousarray(np.asarray(a, dtype=np.float32))
    ident, c64, rmask = _consts()
    pv, tk = _pack_params(inputs)
    shared = {"meta": f(inputs["meta_tokens"]), "pvec": pv, "tokp": tk, "ident": ident, "c64": c64, "rmask": rmask}
    for nm in ("ffa_w_gate", "ffa_w_up", "ffa_w_down", "w_in", "w_out", "ffb_w_gate", "ffb_w_up", "ffb_w_down"):
        shared[nm] = f(inputs[nm])
    in_maps = []
    for b in range(8):
        m = dict(shared)
        m["xp"] = f(inputs["x_prompt"][b, :NPTOK])
        m["xs"] = f(inputs["x_sample"][b])
        m["sconv"] = f(inputs["state_conv"][:, b])
        m["sssm"] = f(np.asarray(inputs["state_ssm"])[:, b].reshape(2, 512, 128))
        m["shg"] = f(inputs["state_hgrn"][:, b])
        in_maps.append(m)
    res = run_bass_kernel_spmd(nc, in_maps, core_ids=list(range(8)))
    R = res.results
    y_prompt = np.stack([R[b]["yp"] for b in range(8)], 0)
    y_sample = np.stack([R[b]["ys"] for b in range(8)], 0)
    outs = [y_prompt, y_sample]
    for q in ("p", "s"):
        conv = np.stack([R[b]["conv_" + q] for b in range(8)], 1)
        ssm = np.stack([R[b]["ssm_" + q].reshape(2, 8, 64, 128) for b in range(8)], 1)
        hg = np.stack([R[b]["hg_" + q] for b in range(8)], 1)
        outs += [conv, ssm, hg]
    return tuple(np.ascontiguousarray(o.astype(np.float32)) for o in outs)


def kernel(**inputs):
    return run(inputs, {"nseg": 8, "depth": 2, "mixer": True, "ffn": True})
```
